# Optimizing a Trainium2 kernel written in Bass

```python
import jax, jax.numpy as jnp
from jax import lax
import numpy as np

D_MODEL = 1024
BATCH = 1
SEQ = 16384
DEPTH = 1
DEC_BATCH = 16
DEC_SEQ = 2048
PAST_LEN = 128

GRID_W = 64
N_MEM = 256
F_GROUPS = 4
F_CH = 64
F_WIDTH = F_GROUPS * F_CH
HEAD_DIM = 128
N_Q_HEADS = 6
N_KV_HEADS = 2
Q_PER_KV = N_Q_HEADS // N_KV_HEADS
Q_WIDTH = N_Q_HEADS * HEAD_DIM
KV_WIDTH = N_KV_HEADS * HEAD_DIM
MIX_WIDTH = F_WIDTH + Q_WIDTH
IN_WIDTH = F_WIDTH + Q_WIDTH + 2 * KV_WIDTH
Q_BLOCK = 128
ROPE_AXIS_DIM = HEAD_DIM // 2
ROPE_THETA = 10000.0
X_HEADS = 4
X_HEAD_DIM = D_MODEL // X_HEADS
D_FF = 4 * D_MODEL
EPS = 1e-6

kernel_name = "hybrid_fourier_gqa_axial_encoder"


def rms_norm(x, g):
    xf = x.astype(jnp.float32)
    y = xf * lax.rsqrt(jnp.mean(xf * xf, axis=-1, keepdims=True) + EPS)
    return (y * g.astype(jnp.float32)).astype(x.dtype)


def axial_angles(n_tok):
    rows = n_tok // GRID_W
    r = jnp.repeat(jnp.arange(rows, dtype=jnp.float32), GRID_W)
    c = jnp.tile(jnp.arange(GRID_W, dtype=jnp.float32), rows)
    inv_freq = 1.0 / (ROPE_THETA ** (jnp.arange(0, ROPE_AXIS_DIM, 2, dtype=jnp.float32) / ROPE_AXIS_DIM))
    ang_r = r[:, None] * inv_freq[None, :]
    ang_c = c[:, None] * inv_freq[None, :]
    return (jnp.cos(ang_r)[:, None, :], jnp.sin(ang_r)[:, None, :],
            jnp.cos(ang_c)[:, None, :], jnp.sin(ang_c)[:, None, :])


def rotate(xp, cos, sin):
    half = xp.shape[-1] // 2
    a, b = xp[..., :half], xp[..., half:]
    return jnp.concatenate([a * cos - b * sin, b * cos + a * sin], axis=-1)


def axial_rope(x, angles):
    cr, sr, cc, sc = angles
    xf = x.astype(jnp.float32)
    out = jnp.concatenate([rotate(xf[..., :ROPE_AXIS_DIM], cr, sr),
                           rotate(xf[..., ROPE_AXIS_DIM:], cc, sc)], axis=-1)
    return out.astype(x.dtype)


def fourier_mixer(zf, w_f):
    b, s, _ = zf.shape
    zg = zf.reshape(b, s, F_GROUPS, F_CH).astype(jnp.float32)
    fr = jnp.fft.fft2(zg, axes=(1, 3), norm="ortho").real.astype(zf.dtype)
    out = jnp.einsum('bsgc,gcd->bsgd', fr, w_f)
    return out.reshape(b, s, F_WIDTH)


def block_gqa(q, k, v):
    b, s, _, d = q.shape
    nb = s // Q_BLOCK
    scale = 1.0 / np.sqrt(d).astype(np.float32)
    qb = q.reshape(b, nb, Q_BLOCK, N_KV_HEADS, Q_PER_KV, d).transpose(1, 0, 2, 3, 4, 5)
    kf = k.astype(jnp.float32)

    def one_block(qblk):
        sc = jnp.einsum('bqkgd,bskd->bkgqs', qblk.astype(jnp.float32), kf) * scale
        p = jax.nn.softmax(sc, axis=-1).astype(v.dtype)
        return jnp.einsum('bkgqs,bskd->bqkgd', p, v)

    ob = lax.map(one_block, qb)
    return ob.transpose(1, 0, 2, 3, 4, 5).reshape(b, s, N_Q_HEADS * d)


def memory_cross_attention(h, mem_n, w_cq, w_ckv, g_cq, g_ck, w_co):
    b, s, _ = h.shape
    m = mem_n.shape[1]
    q = rms_norm((h @ w_cq).reshape(b, s, X_HEADS, X_HEAD_DIM), g_cq)
    kv = mem_n @ w_ckv
    k = rms_norm(kv[..., :D_MODEL].reshape(b, m, X_HEADS, X_HEAD_DIM), g_ck)
    v = kv[..., D_MODEL:].reshape(b, m, X_HEADS, X_HEAD_DIM)
    scale = 1.0 / np.sqrt(X_HEAD_DIM).astype(np.float32)
    sc = jnp.einsum('bqhd,bmhd->bhqm', q.astype(jnp.float32), k.astype(jnp.float32)) * scale
    p = jax.nn.softmax(sc, axis=-1).astype(v.dtype)
    o = jnp.einsum('bhqm,bmhd->bqhd', p, v).reshape(b, s, D_MODEL)
    return o @ w_co


def encoder_layer(x, mem, angles, g_mix, w_in, w_fourier, g_q, g_k, w_out,
                  g_cross, g_mem, w_cq, w_ckv, g_cq, g_ck, w_co, g_mlp, w_up, w_down):
    b, s, _ = x.shape
    h = rms_norm(x, g_mix)
    z = h @ w_in
    zf = z[..., :F_WIDTH]
    zq = z[..., F_WIDTH:F_WIDTH + Q_WIDTH]
    zk = z[..., F_WIDTH + Q_WIDTH:F_WIDTH + Q_WIDTH + KV_WIDTH]
    zv = z[..., F_WIDTH + Q_WIDTH + KV_WIDTH:]
    f_out = fourier_mixer(zf, w_fourier)
    q = axial_rope(rms_norm(zq.reshape(b, s, N_Q_HEADS, HEAD_DIM), g_q), angles)
    k = axial_rope(rms_norm(zk.reshape(b, s, N_KV_HEADS, HEAD_DIM), g_k), angles)
    v = zv.reshape(b, s, N_KV_HEADS, HEAD_DIM)
    a_out = block_gqa(q, k, v)
    x = x + jnp.concatenate([f_out, a_out], axis=-1) @ w_out
    x = x + memory_cross_attention(rms_norm(x, g_cross), rms_norm(mem, g_mem),
                                   w_cq, w_ckv, g_cq, g_ck, w_co)
    u = jax.nn.relu(rms_norm(x, g_mlp) @ w_up)
    x = x + (u * u) @ w_down
    return x


def setup_inputs(seed: int = 0) -> dict:
    key = jax.random.key(seed)
    ks = jax.random.split(key, 24)
    f32 = jnp.float32

    def w(k, shape, fan_in, mult=1.0):
        return jax.random.normal(k, shape, f32) * (mult * fan_in ** -0.5)

    def gain(k, shape):
        return 1.0 + 0.02 * jax.random.normal(k, shape, f32)

    L = DEPTH
    return {
        "x_prompt": jax.random.normal(ks[0], (BATCH, SEQ, D_MODEL), f32),
        "x_sample": jax.random.normal(ks[1], (DEC_BATCH, DEC_SEQ, D_MODEL), f32),
        "mem_prompt": jax.random.normal(ks[2], (BATCH, N_MEM, D_MODEL), f32),
        "mem_sample": jax.random.normal(ks[3], (DEC_BATCH, N_MEM, D_MODEL), f32),
        "g_mix": gain(ks[4], (L, D_MODEL)),
        "w_in": w(ks[5], (L, D_MODEL, IN_WIDTH), D_MODEL),
        "w_fourier": w(ks[6], (L, F_GROUPS, F_CH, F_CH), F_CH),
        "g_q": gain(ks[7], (L, HEAD_DIM)),
        "g_k": gain(ks[8], (L, HEAD_DIM)),
        "w_out": w(ks[9], (L, MIX_WIDTH, D_MODEL), MIX_WIDTH, 0.5),
        "g_cross": gain(ks[10], (L, D_MODEL)),
        "g_mem": gain(ks[11], (L, D_MODEL)),
        "w_cq": w(ks[12], (L, D_MODEL, D_MODEL), D_MODEL),
        "w_ckv": w(ks[13], (L, D_MODEL, 2 * D_MODEL), D_MODEL),
        "g_cq": gain(ks[14], (L, X_HEAD_DIM)),
        "g_ck": gain(ks[15], (L, X_HEAD_DIM)),
        "w_co": w(ks[16], (L, D_MODEL, D_MODEL), D_MODEL, 0.5),
        "g_mlp": gain(ks[17], (L, D_MODEL)),
        "w_up": w(ks[18], (L, D_MODEL, D_FF), D_MODEL),
        "w_down": w(ks[19], (L, D_FF, D_MODEL), D_FF, 0.5),
    }


def reference(x_prompt, x_sample, mem_prompt, mem_sample, g_mix, w_in, w_fourier, g_q, g_k, w_out,
              g_cross, g_mem, w_cq, w_ckv, g_cq, g_ck, w_co, g_mlp, w_up, w_down):
    ang_p = axial_angles(x_prompt.shape[1])
    ang_s = axial_angles(x_sample.shape[1])
    yp = x_prompt
    ys = x_sample
    for l in range(DEPTH):
        p = (g_mix[l], w_in[l], w_fourier[l], g_q[l], g_k[l], w_out[l], g_cross[l], g_mem[l],
             w_cq[l], w_ckv[l], g_cq[l], g_ck[l], w_co[l], g_mlp[l], w_up[l], w_down[l])
        yp = encoder_layer(yp, mem_prompt, ang_p, *p)
        ys = encoder_layer(ys, mem_sample, ang_s, *p)
    return (yp, ys)
```

```python
import numpy as np
import ml_dtypes
from contextlib import ExitStack
import concourse.bass as bass
import concourse.mybir as mybir
from concourse.bass_utils import run_bass_kernel_spmd

F32 = mybir.dt.float32
BF16 = mybir.dt.bfloat16
U8 = mybir.dt.uint8
AF = mybir.ActivationFunctionType
ALU = mybir.AluOpType
AX = mybir.AxisListType

P = 128
D = 1024
VW = 136
NCORES = 8
EPS = 1e-6
ENGS = ['pe', 'act', 'dve', 'pool', 'sp']


class Buf:
    def __init__(self, name):
        self.name = name
        self.last_w = None
        self.readers = []
        self.sem = None
        self.dcount = 0
        self.last_dma = None


class Op:
    pass


class StopBuild(Exception):
    pass


import os
KSTOP = os.environ.get('KSTOP', '')
KC_STEP = float(os.environ.get('KC_STEP', '99'))
KC_TILES = int(os.environ.get('KC_TILES', '9999'))
KC_NR = int(os.environ.get('KC_NR', '99'))


def stop_at(name):
    if KSTOP == name:
        raise StopBuild()


class Prog:
    def __init__(self):
        self.ops = {e: [] for e in ENGS}
        self.bar = {e: [] for e in ENGS}
        self.epoch = 0
        self.epoch_used = {'sw': 0, 'hw': 0}
        self.kind_slots = {'sw': [], 'hw': []}
        self.slot_count = []
        self.slot_last = []

    def add(self, eng, fn, reads=(), writes=(), dma=None):
        op = Op()
        op.eng = eng
        op.fn = fn
        op.deps = []
        op.signal = False
        op.dma = dma
        op.idx = len(self.ops[eng])
        op.sig = 0
        op.raw = set()
        for b in reads:
            if b.last_w is not None:
                op.deps.append(b.last_w)
                op.raw.add(id(b.last_w))
        for b in writes:
            if b.last_w is not None:
                op.deps.append(b.last_w)
            op.deps.extend(b.readers)
        for b in reads:
            b.readers.append(op)
        for b in writes:
            b.last_w = op
            b.readers = []
        if dma is not None:
            kind = 'sw' if eng == 'pool' else 'hw'
            key = (self.epoch, kind)
            if not hasattr(dma, 'slots'):
                dma.slots = {}
            if key not in dma.slots:
                pool = self.kind_slots[kind]
                u = self.epoch_used[kind]
                self.epoch_used[kind] = u + 1
                if u == len(pool):
                    pool.append(len(self.slot_count))
                    self.slot_count.append(0)
                    self.slot_last.append(None)
                dma.slots[key] = pool[u]
            sl = dma.slots[key]
            self.slot_count[sl] += 16
            op.slot = sl
            op.dval = self.slot_count[sl]
            self.slot_last[sl] = op
        if self.bar[eng]:
            op.deps.extend(self.bar[eng])
            self.bar[eng] = []
        self.ops[eng].append(op)
        return op

    def barrier(self):
        deps = []
        for e in ENGS:
            for o in reversed(self.ops[e]):
                if o.dma is None:
                    deps.append(o)
                    break
        for kind in ('sw', 'hw'):
            for u in range(self.epoch_used[kind]):
                deps.append(self.slot_last[self.kind_slots[kind][u]])
        for e in ENGS:
            self.bar[e] = self.bar[e] + deps
        self.epoch += 1
        self.epoch_used = {'sw': 0, 'hw': 0}

    def emit(self, nc):
        ksim = bool(os.environ.get('KSIM'))
        for e in ENGS:
            for o in self.ops[e]:
                o.waits = []
                best = {}
                bestd = {}
                same = None
                for d in o.deps:
                    if d is o:
                        continue
                    if d.dma is not None:
                        if d.slot not in bestd or bestd[d.slot].dval < d.dval:
                            bestd[d.slot] = d
                    elif d.eng == o.eng:
                        if e == 'pe':
                            continue
                        if e in ('pool', 'act') and not ksim:
                            continue
                        if (id(d) in o.raw or ksim) and (same is None or same.idx < d.idx):
                            same = d
                    else:
                        if d.eng not in best or best[d.eng].idx < d.idx:
                            best[d.eng] = d
                for d in bestd.values():
                    o.waits.append(('d', d))
                for d in best.values():
                    d.signal = True
                    o.waits.append(('c', d))
                if same is not None and (o.idx - same.idx <= 2 or ksim):
                    same.signal = True
                    o.waits.append(('c', same))
        for e in ENGS:
            c = 0
            for o in self.ops[e]:
                if o.dma is None and o.signal:
                    c += 1
                    o.sig = c
        with ExitStack() as es:
            csem = {e: es.enter_context(nc.semaphore("c_" + e)) for e in ENGS}
            dsem = [es.enter_context(nc.semaphore("d%d" % i)) for i in range(len(self.slot_count))]
            block = es.enter_context(nc.Block())
            self.n_wait = 0

            def run(e, h):
                waited = {}
                for o in self.ops[e]:
                    for kind, d in o.waits:
                        if kind == 'd':
                            s, v, k = dsem[d.slot], d.dval, ('d', d.slot)
                        else:
                            s, v, k = csem[d.eng], d.sig, ('c', d.eng)
                        if waited.get(k, 0) >= v:
                            continue
                        waited[k] = v
                        h.wait_ge(s, v)
                        self.n_wait += 1
                    ins = o.fn(h)
                    if o.dma is not None:
                        ins.then_inc(dsem[o.slot], 16)
                    elif o.signal:
                        ins.then_inc(csem[e], 1)
                if e == 'sp':
                    for i, sm in enumerate(dsem):
                        h.wait_ge(sm, self.slot_count[i])

            @block.tensor
            def _(h):
                run('pe', h)

            @block.scalar
            def _(h):
                run('act', h)

            @block.vector
            def _(h):
                run('dve', h)

            @block.gpsimd
            def _(h):
                run('pool', h)

            @block.sync
            def _(h):
                run('sp', h)
        print("ops:", {e: len(v) for e, v in self.ops.items()}, "waits:", self.n_wait, "dma sems:", len(self.slot_count), flush=True)


class Arena:
    def __init__(self, ap_u8, size):
        self.ar = ap_u8
        self.size = size
        self.off = 0
        self.marks = []

    def push(self):
        self.marks.append(self.off)

    def pop(self):
        self.off = self.marks.pop()

    def alloc(self, shape, dtype):
        esz = 4 if dtype == F32 else 2
        n = 1
        for s in shape[1:]:
            n *= s
        nb = (n * esz + 63) // 64 * 64
        assert self.off + nb <= self.size, ("SBUF arena overflow", self.off, nb, self.size)
        v = self.ar[:, self.off:self.off + n * esz].bitcast(dtype)
        self.off += nb
        if len(shape) == 3:
            v = v.rearrange("p (a b) -> p a b", b=shape[2])
        elif len(shape) == 4:
            v = v.rearrange("p (a b c) -> p a b c", b=shape[2], c=shape[3])
        return v


class Cfg:
    def __init__(self, n1p=128, n1s=16, ns=2):
        self.N1P = n1p
        self.N1S = n1s
        self.NS = ns
        self.LP = n1p // NCORES
        self.SP = 128 * n1p
        self.SS = 128 * n1s
        self.NM = 256


def build_program(cfg, debug=False):
    nc = bass.Bass("TRN2", target_bir_lowering=False)
    N1P, N1S, NS, LP = cfg.N1P, cfg.N1S, cfg.NS, cfg.LP
    SP_, SS_ = cfg.SP, cfg.SS
    NTL = LP + NS * N1S

    def din(name, shape, dt=F32):
        return nc.dram_tensor(name, list(shape), dt, kind="ExternalInput").ap()

    def dscr(name, shape, dt):
        return nc.dram_tensor(name, list(shape), dt, kind="Internal").ap()

    xp = din("xp", [SP_, D])
    xpl = din("xpl", [LP * P, D])
    xs = din("xs", [NS, SS_, D])
    memp = din("memp", [256, D])
    mems = din("mems", [NS, 256, D])
    g_mix = din("g_mix", [D]); g_cross = din("g_cross", [D]); g_mem = din("g_mem", [D]); g_mlp = din("g_mlp", [D])
    g_q = din("g_q", [128]); g_k = din("g_k", [128]); g_cq = din("g_cq", [256]); g_ck = din("g_ck", [256])
    w_in = din("w_in", [D, 1536]); w_f = din("w_f", [256, 64])
    w_out = din("w_out", [D, D]); w_cq = din("w_cq", [D, D]); w_ckv = din("w_ckv", [D, 2 * D]); w_co = din("w_co", [D, D])
    w_up = din("w_up", [D, 4 * D]); w_down = din("w_down", [4 * D, D])
    ropeP = din("ropeP", [SP_, 256]); ropePl = din("ropePl", [LP * P, 256]); ropeS = din("ropeS", [SS_, 256])
    dftP = din("dftP", [SP_, 384], BF16); dftS = din("dftS", [SS_, 384], BF16)
    w2P = din("w2P", [N1P, 2 * LP], BF16); w2S = din("w2S", [N1S, 2 * N1S], BF16)
    identb_d = din("identb", [P, P], BF16); identf_d = din("identf", [P, P])
    bdc_d = din("bdc", [P, P]); bds_d = din("bds", [P, P]); cst_d = din("cst", [P, 16])

    yp = nc.dram_tensor("yp", [LP * P, D], F32, kind="ExternalOutput").ap()
    ys = nc.dram_tensor("ys", [NS, SS_, D], F32, kind="ExternalOutput").ap()

    N1MAX = max(N1P, N1S)
    WFsc = dscr("WFsc", [D, 512], BF16)
    KVsc = dscr("KVsc", [P, N1MAX, 512], BF16)
    Tsc = dscr("Tsc", [P, N1MAX, 512], BF16)
    X2sc = dscr("X2sc", [NTL, P, D], F32)

    ARENA = 206 * 1024
    with nc.sbuf_tensor("arena", [P, ARENA], U8) as ar_t:
        ar_ap = ar_t[:]
    with nc.psum_tensor("psum", [P, 4096], F32) as ps_t:
        ps = ps_t[:]
    A = Arena(ar_ap, ARENA)
    pg = Prog()

    def bank(b, n=1):
        return ps[:, b * 512:(b + n) * 512]

    def bankb(b):
        return ps[:, b * 512:(b + 1) * 512].bitcast(BF16)

    pbufs = [Buf("psb%d" % i) for i in range(8)]

    identb = A.alloc([P, P], BF16); identf = A.alloc([P, P], F32)
    cst = A.alloc([P, 16], F32)
    gk_b = A.alloc([P, 2, 128], F32); gq_b = A.alloc([P, 6, 128], F32)
    gcq_b = A.alloc([P, 4, 256], F32); gck_b = A.alloc([P, 4, 256], F32)
    ones_b = A.alloc([P, 8], BF16)
    bconst = Buf("const")

    def dma(eng, out, in_, reads, writes, sembuf):
        return pg.add(eng, lambda h, o=out, i=in_: h.dma_start(out=o, in_=i), reads=reads, writes=writes, dma=sembuf)

    cb = [Buf("cb%d" % i) for i in range(16)]
    dma('sp', identb, identb_d, [], [cb[0]], cb[0])
    dma('sp', identf, identf_d, [], [cb[1]], cb[1])
    dma('sp', cst, cst_d, [], [cb[2]], cb[2])
    def gain_tile(g_ap, name):
        t = A.alloc([P, D], F32)
        b = Buf(name)
        dma('sp', t, g_ap.partition_broadcast(P), [], [b], b)
        return t, b
    for h in range(2):
        dma('sp', gk_b[:, h, :], g_k.partition_broadcast(P), [], [cb[7]], cb[7])
    for h in range(6):
        dma('sp', gq_b[:, h, :], g_q.partition_broadcast(P), [], [cb[8]], cb[8])
    for h in range(4):
        dma('sp', gcq_b[:, h, :], g_cq.partition_broadcast(P), [], [cb[9]], cb[9])
        dma('sp', gck_b[:, h, :], g_ck.partition_broadcast(P), [], [cb[10]], cb[10])
    pg.add('dve', lambda h: h.memset(ones_b, 1.0), [], [cb[11]])
    CONSTS = cb[:12]
    eps_c = cst[:, 0:1]; mhalf_c = cst[:, 1:2]

    def rstd_ops(ss, n, bufs_r, bufs_w):
        pg.add('pool', lambda h: h.tensor_tensor(out=ss, in0=ss, in1=eps_c.to_broadcast([P, n]), op=ALU.add),
               reads=[cb[2]] + bufs_r, writes=bufs_w)
        pg.add('pool', lambda h: h.tensor_tensor(out=ss, in0=ss, in1=mhalf_c.to_broadcast([P, n]), op=(ALU.add if os.environ.get('KC_X') == '4' else ALU.pow)),
               reads=[cb[2]] + bufs_r, writes=bufs_w)

    A.push()
    bdc = A.alloc([P, P], F32); bds = A.alloc([P, P], F32)
    wfs = A.alloc([P, 2, 64], F32)
    winf = A.alloc([P, 8, 256], F32)
    winfT = A.alloc([P, 2, D], F32)
    rhsq = A.alloc([P, 2, 512], F32)
    wfo = A.alloc([P, 8, 512], BF16)
    b_bd = Buf("bd"); b_wfs = Buf("wfs"); b_winf = Buf("winf"); b_winfT = Buf("winfT"); b_rhsq = Buf("rhsq"); b_wfo = Buf("wfo")
    b_bd2 = Buf("bd2")
    dma('sp', bdc, bdc_d, [], [b_bd], b_bd)
    dma('sp', bds, bds_d, [], [b_bd2], b_bd2)
    dma('sp', wfs, w_f.rearrange("(q p) e -> p q e", p=P), [], [b_wfs], b_wfs)
    dma('sp', winf, w_in.rearrange("(kc p) n -> p kc n", p=P)[:, :, 0:256], [], [b_winf], b_winf)
    pg.add('dve', lambda h: h.memset(rhsq, 0.0), [], [b_rhsq])
    for q in range(2):
        pg.add('pe', lambda h, q=q: h.matmul(bank(0)[:, q * 128:q * 128 + 64], lhsT=bdc, rhs=wfs[:, q, :], start=True, stop=True),
               reads=[b_bd, b_wfs], writes=[pbufs[0]])
        pg.add('pe', lambda h, q=q: h.matmul(bank(0)[:, q * 128 + 64:q * 128 + 128], lhsT=bds, rhs=wfs[:, q, :], start=True, stop=True),
               reads=[b_bd2, b_wfs], writes=[pbufs[0]])
    for q in range(2):
        for gl in range(2):
            g = 2 * q + gl
            msk = cst[:, 2 + gl:3 + gl]
            for ri in range(2):
                pg.add('dve', lambda h, q=q, g=g, ri=ri, msk=msk: h.tensor_scalar(
                    out=rhsq[:, q, ri * 256 + g * 64: ri * 256 + g * 64 + 64],
                    in0=bank(0)[:, q * 128 + ri * 64: q * 128 + ri * 64 + 64],
                    scalar1=msk, scalar2=None, op0=ALU.mult), reads=[pbufs[0], cb[2]], writes=[b_rhsq])
    for kc in range(8):
        for q in range(2):
            pg.add('pe', lambda h, kc=kc, q=q: h.transpose(out=bank(1 + q)[:, 0:128], in_=winf[:, kc, q * 128:(q + 1) * 128], identity=identf),
                   reads=[b_winf, cb[1]], writes=[pbufs[1 + q]])
            pg.add('act', lambda h, kc=kc, q=q: h.copy(out=winfT[:, q, kc * 128:(kc + 1) * 128], in_=bank(1 + q)[:, 0:128]),
                   reads=[pbufs[1 + q]], writes=[b_winfT])
    for kc in range(8):
        pb = 3 + (kc % 2)
        for q in range(2):
            pg.add('pe', lambda h, kc=kc, q=q, pb=pb: h.matmul(bank(pb), lhsT=winfT[:, q, kc * 128:(kc + 1) * 128], rhs=rhsq[:, q, :],
                                                           start=(q == 0), stop=(q == 1)),
                   reads=[b_winfT, b_rhsq], writes=[pbufs[pb]])
        pg.add('act', lambda h, kc=kc, pb=pb: h.copy(out=wfo[:, kc, :], in_=bank(pb)), reads=[pbufs[pb]], writes=[b_wfo])
    dma('pool', WFsc.rearrange("(kc p) n -> p kc n", p=P), wfo, [b_wfo], [], b_wfo)
    pg.barrier()
    if KSTOP == 'W0':
        pg.emit(nc)
        return nc
    A.pop()

    seqs = []
    seqs.append(dict(kind='P', N1=N1P, NL=LP, xctx=xp.rearrange("(p j) d -> j p d", j=N1P),
                     xloc=xpl.rearrange("(i p) d -> i p d", p=P),
                     rope_ctx=ropeP.rearrange("(p j) d -> j p d", j=N1P),
                     rope_loc=ropePl.rearrange("(i p) d -> i p d", p=P),
                     dft=dftP.rearrange("(p j) d -> j p d", j=N1P), w2=w2P, NK1=LP,
                     mem=memp, yout=yp.rearrange("(i p) d -> i p d", p=P), tile0=0))
    for s in range(NS):
        seqs.append(dict(kind='S', N1=N1S, NL=N1S, xctx=xs[s].rearrange("(p j) d -> j p d", j=N1S),
                         xloc=xs[s].rearrange("(p j) d -> j p d", j=N1S),
                         rope_ctx=ropeS.rearrange("(p j) d -> j p d", j=N1S),
                         rope_loc=ropeS.rearrange("(p j) d -> j p d", j=N1S),
                         dft=dftS.rearrange("(p j) d -> j p d", j=N1S), w2=w2S, NK1=N1S,
                         mem=mems[s], yout=ys[s].rearrange("(p j) d -> j p d", j=N1S), tile0=LP + s * N1S))

    INV32 = 1.0 / 32.0

    def load_w_bf16(dst, src_ap, buf, nsplit=1):
        kcs = dst.shape[1]
        v = src_ap.rearrange("(kc p) n -> p kc n", p=P)
        step = kcs // nsplit
        for i in range(nsplit):
            dma('pool', dst[:, i * step:(i + 1) * step, :], v[:, i * step:(i + 1) * step, :], [], [buf], buf)

    def rmsnorm_tile(xt, gb, xh, junk, ss, b_x, b_xh, b_junk, b_ss, gbuf):
        pg.add('act', lambda h: h.activation(out=junk, in_=xt, func=AF.Square, scale=INV32, accum_out=ss),
               reads=[b_x], writes=[b_junk, b_ss])
        rstd_ops(ss, 1, [b_ss], [b_ss])
        pg.add('dve', lambda h: h.scalar_tensor_tensor(out=xh, in0=xt, scalar=ss, in1=gb, op0=ALU.mult, op1=ALU.mult),
               reads=[b_x, b_ss, gbuf], writes=[b_xh])

    def transpose8(src, b_src, pbank, dstT, b_dst, n=8, eng='act'):
        pv = bankb(pbank)
        for kc in range(n):
            pg.add('pe', lambda h, kc=kc: h.transpose(out=pv[:, kc * 128:(kc + 1) * 128], in_=src[:, kc * 128:(kc + 1) * 128], identity=identb),
                   reads=[b_src, cb[0]], writes=[pbufs[pbank]])
        if eng == 'act':
            pg.add('act', lambda h: h.copy(out=dstT, in_=pv[:, 0:n * 128].rearrange("p (a b) -> p a b", b=128)), reads=[pbufs[pbank]], writes=[b_dst])
        else:
            pg.add('dve', lambda h: h.tensor_copy(out=dstT, in_=pv[:, 0:n * 128].rearrange("p (a b) -> p a b", b=128)), reads=[pbufs[pbank]], writes=[b_dst])

    def normrope(zsrc, b_z, H, gain_b, gbuf, rt, b_rt, tmp, b_tmp, kr, b_kr):
        sq, kn, t1, t2, ssq = tmp
        z3 = zsrc.rearrange("p (h d) -> p h d", d=128)
        KX = os.environ.get('KC_X', '')
        if KX == '2':
            pg.add('act', lambda h: h.activation(out=sq[:, 0:H, :], in_=z3, func=AF.Square, scale=float(1.0 / np.sqrt(128.0))),
                   reads=list(b_z), writes=[b_tmp[0], b_tmp[4]])
        else:
            for hh in range(H):
                pg.add('act', lambda h, hh=hh: h.activation(out=sq[:, hh, :], in_=z3[:, hh, :], func=AF.Square, scale=float(1.0 / np.sqrt(128.0)), accum_out=ssq[:, hh, 0:1]),
                       reads=list(b_z), writes=[b_tmp[0], b_tmp[4]])
        if KX != '1':
            for hh in range(H):
                rstd_ops(ssq[:, hh, 0:1], 1, [b_tmp[4]], [b_tmp[4]])
        if KC_NR < 3:
            return
        pg.add('dve', lambda h: h.tensor_tensor(out=kn[:, 0:H, :], in0=z3, in1=gain_b, op=ALU.mult),
               reads=list(b_z) + [gbuf, b_tmp[0]], writes=[b_tmp[1]])
        pg.add('dve', lambda h: h.tensor_tensor(out=kn[:, 0:H, :], in0=kn[:, 0:H, :], in1=ssq[:, 0:H, 0:1].to_broadcast([P, H, 128]), op=ALU.mult),
               reads=[b_tmp[4], b_tmp[1]], writes=[b_tmp[1]])
        if KC_NR < 5:
            return
        C_b = rt[:, 0:128].unsqueeze(1).to_broadcast([P, H, 128])
        pg.add('dve', lambda h: h.tensor_tensor(out=t1[:, 0:H, :], in0=kn[:, 0:H, :], in1=C_b, op=ALU.mult),
               reads=[b_tmp[1], b_rt], writes=[b_tmp[2]])
        kn5 = kn[:, 0:H, :].rearrange("p h (a t k) -> p h a t k", a=2, t=2)
        t25 = t2[:, 0:H, :].rearrange("p h (a t k) -> p h a t k", a=2, t=2)
        S5 = rt[:, 128:256].rearrange("p (a t k) -> p a t k", a=2, t=2)
        for a in range(2):
            for t in range(2):
                pg.add('dve', lambda h, a=a, t=t: h.tensor_tensor(out=t25[:, :, a, t, :], in0=kn5[:, :, a, 1 - t, :],
                                                                 in1=S5[:, a, t, :].unsqueeze(1).to_broadcast([P, H, 32]), op=ALU.mult),
                       reads=[b_tmp[1], b_rt], writes=[b_tmp[3]])
        pg.add('dve', lambda h: h.tensor_tensor(out=kr[:, 0:H, :], in0=t1[:, 0:H, :], in1=t2[:, 0:H, :], op=ALU.add),
               reads=[b_tmp[2], b_tmp[3]], writes=[b_kr])

    def do_seq(si, sq_):
        N1 = sq_['N1']; NL = sq_['NL']; kind = sq_['kind']
        NLOC = NL * P
        A.push()
        QT = A.alloc([P, 6, NLOC], BF16)
        foT = A.alloc([P, 2, NLOC], BF16)
        QB = min(512, NLOC)
        NQB = NLOC // QB
        b_QT = [[Buf("QT%d_%d" % (h, qb)) for qb in range(NQB)] for h in range(6)]
        b_foT = Buf("foT")

        A.push()
        Wctx = A.alloc([P, 8, 1024], BF16)
        Wq = A.alloc([P, 8, 768], BF16)
        b_Wctx = Buf("Wctx"); b_Wq = Buf("Wq"); b_Wctx2 = Buf("Wctx2")
        dma('sp', Wctx[:, :, 0:512], WFsc.rearrange("(kc p) n -> p kc n", p=P), [], [b_Wctx], b_Wctx)
        dma('pool', Wctx[:, :, 512:1024], w_in.rearrange("(kc p) n -> p kc n", p=P)[:, :, 1024:1536], [], [b_Wctx2], b_Wctx2)
        dma('pool', Wq, w_in.rearrange("(kc p) n -> p kc n", p=P)[:, :, 256:1024], [], [b_Wq], b_Wq)
        NSL = 2
        gmix_b, b_gmix = gain_tile(g_mix, "gmix")
        xt = [A.alloc([P, D], F32) for _ in range(3)]; b_xt = [Buf("xt%d" % i) for i in range(3)]
        rt = [A.alloc([P, 256], F32) for _ in range(NSL)]; b_rt = [Buf("rt%d" % i) for i in range(NSL)]
        ft = [A.alloc([P, 384], BF16) for _ in range(NSL)]; b_ft = [Buf("ft%d" % i) for i in range(NSL)]
        junk = A.alloc([P, D], BF16); b_junk = Buf("junk")
        ss = [A.alloc([P, 1], F32) for _ in range(NSL)]; b_ss = [Buf("ss%d" % i) for i in range(NSL)]
        xh = [A.alloc([P, D], BF16) for _ in range(NSL)]; b_xh = [Buf("xh%d" % i) for i in range(NSL)]
        xhT = [A.alloc([P, 8, 128], BF16) for _ in range(NSL)]; b_xhT = [Buf("xhT%d" % i) for i in range(NSL)]
        Yt = [A.alloc([P, 512], BF16) for _ in range(NSL)]; b_Yt = [Buf("Yt%d" % i) for i in range(NSL)]
        KVt = [A.alloc([P, 512], BF16) for _ in range(NSL)]; b_KVt = [Buf("KVt%d" % i) for i in range(NSL)]
        Tt = [A.alloc([P, 512], BF16) for _ in range(NSL)]; b_Tt = [Buf("Tt%d" % i) for i in range(NSL)]
        tmpk = (A.alloc([P, 2, 128], F32), A.alloc([P, 2, 128], F32), A.alloc([P, 2, 128], F32), A.alloc([P, 2, 128], F32), A.alloc([P, 8, 16], F32))
        b_tmpk = [Buf("tmpk%d" % i) for i in range(5)]
        tmpq = (A.alloc([P, 6, 128], F32), A.alloc([P, 6, 128], F32), A.alloc([P, 6, 128], F32), A.alloc([P, 6, 128], F32), A.alloc([P, 8, 16], F32))
        b_tmpq = [Buf("tmpq%d" % i) for i in range(5)]
        krk = A.alloc([P, 2, 128], BF16); b_krk = Buf("krk")
        krq = A.alloc([P, 6, 128], BF16); b_krq = Buf("krq")

        rt3 = [A.alloc([P, 256], F32) for _ in range(3)]; b_rt3 = [Buf("rt3_%d" % i) for i in range(3)]
        ft3 = [A.alloc([P, 384], BF16) for _ in range(3)]; b_ft3 = [Buf("ft3_%d" % i) for i in range(3)]

        def ctx_loads(j, it, do_ctx, do_q, xsrc, ropesrc):
            s3 = it % 3
            dma('sp', xt[s3], xsrc[j], [], [b_xt[s3]], b_xt[s3])
            dma('sp', rt3[s3], ropesrc[j], [], [b_rt3[s3]], b_rt3[s3])
            if do_ctx:
                dma('sp', ft3[s3], sq_['dft'][j], [], [b_ft3[s3]], b_ft3[s3])

        def ctx_part1(j, it, do_ctx, do_q, xsrc, ropesrc, half=0):
            s3 = it % 3; s2 = it % NSL
            if do_ctx and do_q:
                zc, zq = 0, 2
            elif do_ctx:
                zc, zq = 2 * (it % 2), None
            else:
                zc, zq = None, 2 * (it % 2)
            if half in (0, 1):
                rmsnorm_tile(xt[s3], gmix_b, xh[s2], junk, ss[s2], b_xt[s3], b_xh[s2], b_junk, b_ss[s2], b_gmix)
                transpose8(xh[s2], b_xh[s2], 4 + s2, xhT[s2], b_xhT[s2])
            if half == 1:
                return zc, zq
            if do_ctx:
                for nb in range(2):
                    for kc in range(8):
                        pg.add('pe', lambda h, nb=nb, kc=kc: h.matmul(bank(zc + nb), lhsT=xhT[s2][:, kc, :], rhs=Wctx[:, kc, nb * 512:(nb + 1) * 512],
                                                                      start=(kc == 0), stop=(kc == 7)),
                               reads=[b_xhT[s2], b_Wctx, b_Wctx2], writes=[pbufs[zc + nb]])
            if do_q:
                for nb in range(2):
                    w = 512 if nb == 0 else 256
                    for kc in range(8):
                        pg.add('pe', lambda h, nb=nb, kc=kc, w=w: h.matmul(bank(zq + nb)[:, 0:w], lhsT=xhT[s2][:, kc, :], rhs=Wq[:, kc, nb * 512:nb * 512 + w],
                                                                           start=(kc == 0), stop=(kc == 7)),
                               reads=[b_xhT[s2], b_Wq], writes=[pbufs[zq + nb]])
            return zc, zq

        def ctx_part2(j, it, do_ctx, do_q, xsrc, ropesrc, zc, zq):
            s3 = it % 3; s2 = it % NSL
            rt_ = rt3[s3]; b_rt_ = b_rt3[s3]
            if do_ctx:
                pg.add('act', lambda h: h.copy(out=Yt[s2], in_=bank(zc)), reads=[pbufs[zc]], writes=[b_Yt[s2]])
                pg.add('act', lambda h: h.copy(out=KVt[s2][:, 256:512], in_=bank(zc + 1)[:, 256:512]),
                       reads=[pbufs[zc + 1]], writes=[b_KVt[s2]])
                normrope(bank(zc + 1)[:, 0:256], [pbufs[zc + 1]], 2, gk_b, cb[7], rt_, b_rt_, tmpk, b_tmpk, krk, b_krk)
                pv = bankb(7)
                for hh in range(2):
                    pg.add('pe', lambda h, hh=hh: h.transpose(out=pv[:, hh * 128:(hh + 1) * 128], in_=krk[:, hh, :], identity=identb),
                           reads=[b_krk, cb[0]], writes=[pbufs[7]])
                pg.add('act', lambda h: h.copy(out=KVt[s2][:, 0:256], in_=pv[:, 0:256]), reads=[pbufs[7]], writes=[b_KVt[s2]])
                dma('sp', KVsc[:, j, :], KVt[s2], [b_KVt[s2]], [], b_KVt[s2])
                Fr = ft3[s3][:, 0:128]; Fi = ft3[s3][:, 128:256]; nFi = ft3[s3][:, 256:384]
                mm = [(0, Fr, 0, True, False), (0, nFi, 256, False, True), (256, Fi, 0, True, False), (256, Fr, 256, False, True)]
                for (oc, lw, ic, st, sp_) in mm:
                    pg.add('pe', lambda h, oc=oc, lw=lw, ic=ic, st=st, sp_=sp_: h.matmul(bank(6)[:, oc:oc + 256], lhsT=lw, rhs=Yt[s2][:, ic:ic + 256], start=st, stop=sp_),
                           reads=[b_ft3[s3], b_Yt[s2]], writes=[pbufs[6]])
                pg.add('act', lambda h: h.copy(out=Tt[s2], in_=bank(6)), reads=[pbufs[6]], writes=[b_Tt[s2]])
                dma('sp', Tsc[:, j, :], Tt[s2], [b_Tt[s2]], [], b_Tt[s2])
            if do_q:
                normrope(bank(zq, 2)[:, 0:768], [pbufs[zq], pbufs[zq + 1]], 6, gq_b, cb[8], rt_, b_rt_, tmpq, b_tmpq, krq, b_krq)
                pv = bankb(7)
                qb_i = (j * P) // QB
                for hh in range(6):
                    if hh % 4 == 0:
                        cnt = min(4, 6 - hh)
                    pg.add('pe', lambda h, hh=hh: h.transpose(out=pv[:, (hh % 4) * 128:(hh % 4 + 1) * 128], in_=krq[:, hh, :], identity=identb),
                           reads=[b_krq, cb[0]], writes=[pbufs[7]])
                    if hh % 4 == cnt - 1:
                        h0 = hh - (cnt - 1)
                        pg.add('act', lambda h, h0=h0, cnt=cnt: h.copy(out=QT[:, h0:h0 + cnt, j * P:(j + 1) * P],
                                                                       in_=pv[:, 0:cnt * 128].rearrange("p (a b) -> p a b", b=128)),
                               reads=[pbufs[7]], writes=[b_QT[hq][qb_i] for hq in range(h0, h0 + cnt)])

        tiles = []
        if kind == 'P':
            for j in range(min(N1, KC_TILES)):
                tiles.append((j, len(tiles), True, False, sq_['xctx'], sq_['rope_ctx']))
            for i in range(min(NL, KC_TILES)):
                tiles.append((i, len(tiles), False, True, sq_['xloc'], sq_['rope_loc']))
        else:
            for j in range(N1):
                tiles.append((j, len(tiles), True, False, sq_['xctx'], sq_['rope_ctx']))
                tiles.append((j, len(tiles), False, True, sq_['xctx'], sq_['rope_ctx']))
        nt = len(tiles)
        if True:
            ctx_loads(*tiles[0])
            if nt > 1:
                ctx_loads(*tiles[1])
            zz = {0: ctx_part1(*tiles[0])}
            for ti in range(nt):
                if ti + 2 < nt:
                    ctx_loads(*tiles[ti + 2])
                if ti + 1 < nt:
                    zz[ti + 1] = ctx_part1(*tiles[ti + 1], half=1)
                ctx_part2(*tiles[ti], *zz[ti])
                if ti + 1 < nt:
                    ctx_part1(*tiles[ti + 1], half=2)
        else:
            ctx_loads(*tiles[0])
            for ti in range(nt):
                if ti + 1 < nt:
                    ctx_loads(*tiles[ti + 1])
                z_ = ctx_part1(*tiles[ti])
                ctx_part2(*tiles[ti], *z_)
        pg.barrier()
        stop_at('C')
        A.pop()

        A.push()
        NK1 = sq_['NK1']
        w2t = A.alloc([P, 2 * NK1], BF16); b_w2 = Buf("w2")
        dma('sp', w2t[0:N1, :], sq_['w2'], [], [b_w2], b_w2)
        G8 = 8
        tl = [A.alloc([P, G8, 512], BF16) for _ in range(3)]; b_tl = [Buf("tl%d" % i) for i in range(3)]
        K2G = min(128, 512 // NK1)
        for g8 in range(128 // G8):
            s = g8 % 3
            dma('sp', tl[s][0:N1], Tsc[g8 * G8:(g8 + 1) * G8, 0:N1, :].rearrange("k j c -> j k c"), [], [b_tl[s]], b_tl[s])
            for kk in range(G8):
                k2 = g8 * G8 + kk
                grp = k2 // K2G; pos = k2 % K2G
                for ch in range(2):
                    pb = 2 * (grp % 2) + ch
                    o = bank(pb)[:, pos * NK1:(pos + 1) * NK1]
                    pg.add('pe', lambda h, o=o, s=s, kk=kk, ch=ch: h.matmul(o, lhsT=tl[s][0:N1, kk, ch * 128:(ch + 1) * 128], rhs=w2t[0:N1, 0:NK1], start=True, stop=False),
                           reads=[b_tl[s], b_w2], writes=[pbufs[pb]])
                    pg.add('pe', lambda h, o=o, s=s, kk=kk, ch=ch: h.matmul(o, lhsT=tl[s][0:N1, kk, 256 + ch * 128:256 + (ch + 1) * 128], rhs=w2t[0:N1, NK1:2 * NK1], start=False, stop=True),
                           reads=[b_tl[s], b_w2], writes=[pbufs[pb]])
                if pos == K2G - 1:
                    for ch in range(2):
                        pb = 2 * (grp % 2) + ch
                        src = bank(pb)[:, 0:K2G * NK1]
                        if kind == 'P':
                            dst = foT[:, ch, :].rearrange("p (k1 k2) -> p k1 k2", k2=128)[:, :, grp * K2G:(grp + 1) * K2G]
                            sv = src.rearrange("p (k2 k1) -> p k1 k2", k1=NK1)
                        else:
                            AA = K2G // N1
                            dst = foT[:, ch, :].rearrange("p (jj k1 a) -> p jj k1 a", k1=NK1, a=128 // N1)[:, :, :, grp * AA:(grp + 1) * AA]
                            sv = src.rearrange("p (a jj k1) -> p jj k1 a", jj=N1, k1=NK1)
                        pg.add('act', lambda h, dst=dst, sv=sv: h.copy(out=dst, in_=sv), reads=[pbufs[pb]], writes=[b_foT])
        pg.barrier()
        stop_at('F2')
        A.pop()

        A.push()
        NKC = 8 if N1 >= 16 else 1
        CH = N1 // NKC
        KTg = A.alloc([P, N1, 128], BF16); Vg = A.alloc([P, N1, 128], BF16)
        stg = [A.alloc([P, CH, 512], BF16) for _ in range(2)]; b_stg = [Buf("stg%d" % i) for i in range(2)]
        b_KT = [Buf("KT%d" % i) for i in range(NKC)]; b_V = [Buf("V%d" % i) for i in range(NKC)]
        NPT = 4
        PT = [A.alloc([P, 2, QB], BF16) for _ in range(NPT)]; b_PT = [Buf("PT%d" % i) for i in range(NPT)]
        racc = [A.alloc([P, 2, QB], F32) for _ in range(2)]; b_racc = [Buf("racc%d" % i) for i in range(2)]
        rs = A.alloc([P, QB], F32); b_rs = Buf("rs")
        rinvb = A.alloc([P, QB], F32); b_rinvb = Buf("rinvb")
        ones_f = A.alloc([P, P], F32); b_ones = Buf("ones_f")
        pg.add('dve', lambda h: h.memset(ones_f, 1.0), [], [b_ones])
        scale = float(1.0 / np.sqrt(128.0))
        b_S = [Buf("Sp0"), Buf("Sp1")]
        b_accO = [Buf("accO0"), Buf("accO1")]
        b_rsb = [Buf("rsb0"), Buf("rsb1")]
        items = []
        blk = 0
        for g in range(2):
            for hl in range(3):
                for qb in range(NQB):
                    for jp in range(N1 // 2):
                        items.append(dict(g=g, hq=3 * g + hl, qb=qb, jp=jp, aset=blk % 2, newg=(hl == 0 and qb == 0 and jp == 0),
                                          last=(jp == N1 // 2 - 1)))
                    blk += 1
        cnt = {'sit': 0, 'pit': 0}
        pend = []

        def emit_S(it):
            g = it['g']; hq = it['hq']; qb = it['qb']; jp = it['jp']
            if it['newg']:
                for c4 in range(NKC):
                    st_ = (g * NKC + c4) % 2
                    dma('sp', stg[st_], KVsc[:, c4 * CH:(c4 + 1) * CH, :], [], [b_stg[st_]], b_stg[st_])
                    pg.add('pool', lambda h, c4=c4, st_=st_, g=g: h.tensor_copy(out=KTg[:, c4 * CH:(c4 + 1) * CH, :], in_=stg[st_][:, :, g * 128:(g + 1) * 128]),
                           reads=[b_stg[st_]], writes=[b_KT[c4]])
                    pg.add('dve', lambda h, c4=c4, st_=st_, g=g: h.tensor_copy(out=Vg[:, c4 * CH:(c4 + 1) * CH, :], in_=stg[st_][:, :, 256 + g * 128:256 + (g + 1) * 128]),
                           reads=[b_stg[st_]], writes=[b_V[c4]])
            sp_ = cnt['sit'] % 2; cnt['sit'] += 1
            it['sp'] = sp_
            qcols = QT[:, hq, qb * QB:(qb + 1) * QB]
            for u in range(2):
                j = 2 * jp + u
                pg.add('pe', lambda h, sp_=sp_, u=u, j=j, qcols=qcols: h.matmul(bank(2 * sp_ + u)[:, 0:QB], lhsT=KTg[:, j, :], rhs=qcols, start=True, stop=True),
                       reads=[b_KT[j // CH], b_QT[hq][qb]], writes=[b_S[sp_]])

        def emit_rest(it):
            g = it['g']; hq = it['hq']; qb = it['qb']; jp = it['jp']; aset = it['aset']; sp_ = it['sp']
            pt = cnt['pit'] % NPT; cnt['pit'] += 1
            qcols = QT[:, hq, qb * QB:(qb + 1) * QB]
            accO = bank(4 + aset)[:, 0:QB]
            sview = ps[:, (2 * sp_) * 512:(2 * sp_ + 2) * 512].rearrange("p (u c) -> p u c", u=2)[:, :, 0:QB]
            pg.add('act', lambda h, sview=sview, pt=pt: h.activation(out=PT[pt], in_=sview, func=AF.Exp, scale=scale),
                   reads=[b_S[sp_]], writes=[b_PT[pt]])
            for u in range(2):
                j = 2 * jp + u
                pg.add('pe', lambda h, pt=pt, u=u, j=j, accO=accO: h.matmul(accO, lhsT=Vg[:, j, :], rhs=PT[pt][:, u, :],
                                                                          start=(j == 0), stop=(j == N1 - 1)),
                       reads=[b_PT[pt], b_V[j // CH]], writes=[b_accO[aset]])
            if jp == 0:
                pg.add('dve', lambda h, pt=pt, aset=aset: h.tensor_copy(out=racc[aset], in_=PT[pt]),
                       reads=[b_PT[pt]], writes=[b_racc[aset]])
            else:
                pg.add('dve', lambda h, pt=pt, aset=aset: h.tensor_tensor(out=racc[aset], in0=racc[aset], in1=PT[pt], op=ALU.add),
                       reads=[b_PT[pt], b_racc[aset]], writes=[b_racc[aset]])
            if pend:
                pend.pop()()

            def epilogue(aset=aset, accO=accO, qcols=qcols, hq=hq, qb=qb):
                pg.add('dve', lambda h: h.tensor_tensor(out=rs, in0=racc[aset][:, 0, :], in1=racc[aset][:, 1, :], op=ALU.add),
                       reads=[b_racc[aset]], writes=[b_rs])
                rsb = bank(6 + aset)[:, 0:QB]
                pg.add('pe', lambda h: h.matmul(rsb, lhsT=ones_f, rhs=rs, start=True, stop=True),
                       reads=[b_rs, b_ones], writes=[b_rsb[aset]])
                pg.add('dve', lambda h: h.reciprocal(out=rinvb, in_=rsb), reads=[b_rsb[aset]], writes=[b_rinvb])
                pg.add('dve', lambda h: h.tensor_tensor(out=qcols, in0=accO, in1=rinvb, op=ALU.mult),
                       reads=[b_accO[aset], b_rinvb], writes=[b_QT[hq][qb]])
            if it['last']:
                pend.append(epilogue)

        for i_, it_ in enumerate(items):
            if 'sp' not in it_:
                emit_S(it_)
            if i_ + 1 < len(items) and not items[i_ + 1]['newg']:
                emit_S(items[i_ + 1])
            emit_rest(it_)
        if pend:
            pend.pop()()
        pg.barrier()
        stop_at('A')
        A.pop()

        A.push()
        kcT = A.alloc([P, 8, 256], BF16); vc = A.alloc([P, 2, 4, 257], BF16)
        b_kcT = Buf("kcT"); b_vc = Buf("vc")
        A.push()
        wckv = A.alloc([P, 8, 2048], BF16); b_wckv = Buf("wckv")
        load_w_bf16(wckv, w_ckv, b_wckv, nsplit=1)
        gmem_b, b_gmem = gain_tile(g_mem, "gmem")
        mt = A.alloc([P, D], F32); b_mt = Buf("mt")
        mjunk = A.alloc([P, D], BF16); b_mjunk = Buf("mjunk")
        mss = A.alloc([P, 1], F32); b_mss = Buf("mss")
        mh = A.alloc([P, D], BF16); b_mh = Buf("mh")
        mhT = A.alloc([P, 8, 128], BF16); b_mhT = Buf("mhT")
        ksq = A.alloc([P, 4, 256], F32); b_ksq = Buf("ksq")
        kss = A.alloc([P, 4], F32); b_kss = Buf("kss")
        kcn = A.alloc([P, 4, 256], F32); b_kcn = Buf("kcn")
        kcb = A.alloc([P, D], BF16); b_kcb = Buf("kcb")
        kcTt = A.alloc([P, 8, 128], BF16); b_kcTt = Buf("kcTt")
        pg.add('dve', lambda h: h.memset(vc[:, :, :, 256:257], 1.0), [], [b_vc])
        for mc in range(2):
            dma('sp', mt, sq_['mem'][mc * P:(mc + 1) * P, :], [], [b_mt], b_mt)
            rmsnorm_tile(mt, gmem_b, mh, mjunk, mss, b_mt, b_mh, b_mjunk, b_mss, b_gmem)
            transpose8(mh, b_mh, 4, mhT, b_mhT)
            for nb in range(4):
                for kc in range(8):
                    pg.add('pe', lambda h, nb=nb, kc=kc: h.matmul(bank(nb), lhsT=mhT[:, kc, :], rhs=wckv[:, kc, nb * 512:(nb + 1) * 512], start=(kc == 0), stop=(kc == 7)),
                           reads=[b_mhT, b_wckv], writes=[pbufs[nb]])
            pg.add('act', lambda h, mc=mc: h.copy(out=vc[:, mc, :, 0:256], in_=bank(2, 2).rearrange("p (h d) -> p h d", d=256)), reads=[pbufs[2], pbufs[3]], writes=[b_vc])
            k3 = bank(0, 2).rearrange("p (h d) -> p h d", d=256)
            for hh in range(4):
                pg.add('act', lambda h, k3=k3, hh=hh: h.activation(out=ksq[:, hh, :], in_=k3[:, hh, :], func=AF.Square, scale=1.0 / 16.0, accum_out=kss[:, hh:hh + 1]),
                       reads=[pbufs[0], pbufs[1]], writes=[b_ksq, b_kss])
            for hh in range(4):
                rstd_ops(kss[:, hh:hh + 1], 1, [b_kss], [b_kss])
            pg.add('dve', lambda h, k3=k3: h.tensor_tensor(out=kcn, in0=k3, in1=gck_b, op=ALU.mult), reads=[pbufs[0], pbufs[1], cb[10], b_ksq], writes=[b_kcn])
            pg.add('dve', lambda h: h.tensor_tensor(out=kcb.rearrange("p (h d) -> p h d", d=256), in0=kcn, in1=kss.unsqueeze(2).to_broadcast([P, 4, 256]), op=ALU.mult),
                   reads=[b_kcn, b_kss], writes=[b_kcb])
            transpose8(kcb, b_kcb, 5, kcTt, b_kcTt)
            pg.add('dve', lambda h, mc=mc: h.tensor_copy(out=kcT[:, :, mc * 128:(mc + 1) * 128], in_=kcTt), reads=[b_kcTt], writes=[b_kcT])
        pg.barrier()
        stop_at('M')
        A.pop()

        wo = A.alloc([P, 8, D], BF16); wcq = A.alloc([P, 8, D], BF16); wco = A.alloc([P, 8, D], BF16)
        b_wo = Buf("wo"); b_wcq = Buf("wcq"); b_wco = Buf("wco")
        load_w_bf16(wo, w_out, b_wo); load_w_bf16(wcq, w_cq, b_wcq); load_w_bf16(wco, w_co, b_wco)
        NS2 = 2
        gcross_b, b_gcross = gain_tile(g_cross, "gcross")
        xt1 = [A.alloc([P, D], F32) for _ in range(NS2)]; b_xt1 = [Buf("xt1_%d" % i) for i in range(NS2)]
        x1 = [A.alloc([P, D], F32) for _ in range(NS2)]; b_x1 = [Buf("x1_%d" % i) for i in range(NS2)]
        x2 = [A.alloc([P, D], F32) for _ in range(NS2)]; b_x2 = [Buf("x2_%d" % i) for i in range(NS2)]
        pj = A.alloc([P, D], BF16); b_pj = Buf("pj")
        pss = [A.alloc([P, 1], F32) for _ in range(NS2)]; b_pss = [Buf("pss%d" % i) for i in range(NS2)]

        def two(shape, dt, name):
            return [A.alloc(shape, dt) for _ in range(2)], [Buf("%s%d" % (name, i)) for i in range(2)]
        xh1, b_xh1 = two([P, D], BF16, "xh1"); xh1T, b_xh1T = two([P, 8, 128], BF16, "xh1T")
        qsq, b_qsq = two([P, 4, 256], F32, "qsq"); qss, b_qss = two([P, 4, 16], F32, "qss")
        qcn, b_qcn = two([P, 4, 256], F32, "qcn"); qcb, b_qcb = two([P, D], BF16, "qcb")
        qcT, b_qcT = two([P, 8, 128], BF16, "qcT"); PTc, b_PTc = two([P, 4, 2, 128], BF16, "PTc")
        rsi, b_rsi = two([P, 4], F32, "rsi"); ocb, b_ocb = two([P, D], BF16, "ocb"); ocT, b_ocT = two([P, 8, 128], BF16, "ocT")
        cscale = 1.0 / 16.0

        def st_load(t, p):
            dma('sp', xt1[p], sq_['xloc'][t], [], [b_xt1[p]], b_xt1[p])

        def st1(t, p):
            X = 2 * p
            tok = slice(t * P, (t + 1) * P)
            for nb in range(2):
                for kc in range(8):
                    if kc < 2:
                        lw = foT[:, kc, tok]; rb = [b_foT]
                    else:
                        lw = QT[:, kc - 2, tok]; rb = [b_QT[kc - 2][(t * P) // QB]]
                    pg.add('pe', lambda h, nb=nb, kc=kc, lw=lw: h.matmul(bank(X + nb), lhsT=lw, rhs=wo[:, kc, nb * 512:(nb + 1) * 512], start=(kc == 0), stop=(kc == 7)),
                           reads=rb + [b_wo], writes=[pbufs[X + nb]])

        def st2(t, p):
            X = 2 * p
            pg.add('dve', lambda h: h.tensor_tensor(out=x1[p], in0=bank(X, 2), in1=xt1[p], op=ALU.add), reads=[pbufs[X], pbufs[X + 1], b_xt1[p]], writes=[b_x1[p]])
            rmsnorm_tile(x1[p], gcross_b, xh1[p], pj, pss[p], b_x1[p], b_xh1[p], b_pj, b_pss[p], b_gcross)

        def st3(t, p):
            transpose8(xh1[p], b_xh1[p], 4 + p, xh1T[p], b_xh1T[p])

        def st4(t, p):
            X = 2 * p
            for nb in range(2):
                for kc in range(8):
                    pg.add('pe', lambda h, nb=nb, kc=kc: h.matmul(bank(X + nb), lhsT=xh1T[p][:, kc, :], rhs=wcq[:, kc, nb * 512:(nb + 1) * 512], start=(kc == 0), stop=(kc == 7)),
                           reads=[b_xh1T[p], b_wcq], writes=[pbufs[X + nb]])

        def st5(t, p):
            X = 2 * p
            q3 = bank(X, 2).rearrange("p (h d) -> p h d", d=256)
            for hh in range(4):
                pg.add('act', lambda h, hh=hh: h.activation(out=qsq[p][:, hh, :], in_=q3[:, hh, :], func=AF.Square, scale=1.0 / 16.0, accum_out=qss[p][:, hh, 0:1]),
                       reads=[pbufs[X], pbufs[X + 1]], writes=[b_qsq[p], b_qss[p]])
            for hh in range(4):
                rstd_ops(qss[p][:, hh, 0:1], 1, [b_qss[p]], [b_qss[p]])
            pg.add('dve', lambda h: h.tensor_tensor(out=qcn[p], in0=q3, in1=gcq_b, op=ALU.mult), reads=[pbufs[X], pbufs[X + 1], cb[9], b_qsq[p]], writes=[b_qcn[p]])
            pg.add('dve', lambda h: h.tensor_tensor(out=qcb[p].rearrange("p (h d) -> p h d", d=256), in0=qcn[p], in1=qss[p][:, :, 0:1].to_broadcast([P, 4, 256]), op=ALU.mult),
                   reads=[b_qcn[p], b_qss[p]], writes=[b_qcb[p]])

        def st6(t, p):
            transpose8(qcb[p], b_qcb[p], 4 + p, qcT[p], b_qcT[p])

        def st7(t, p):
            X = 2 * p
            for hh in range(4):
                for mc in range(2):
                    o = ps[:, X * 512 + (hh * 2 + mc) * 128: X * 512 + (hh * 2 + mc + 1) * 128]
                    for dc in range(2):
                        pg.add('pe', lambda h, o=o, hh=hh, mc=mc, dc=dc: h.matmul(o, lhsT=kcT[:, 2 * hh + dc, mc * 128:(mc + 1) * 128], rhs=qcT[p][:, 2 * hh + dc, :],
                                                                                  start=(dc == 0), stop=(dc == 1)),
                               reads=[b_kcT, b_qcT[p]], writes=[pbufs[X], pbufs[X + 1]])
            pg.add('act', lambda h: h.activation(out=PTc[p], in_=bank(X, 2).rearrange("p (h m t) -> p h m t", m=2, t=128), func=AF.Exp, scale=cscale),
                   reads=[pbufs[X], pbufs[X + 1]], writes=[b_PTc[p]])

        def st8(t, p):
            X = 2 * p; T = 4 + p
            for hh in range(4):
                for mc in range(2):
                    pg.add('pe', lambda h, hh=hh, mc=mc: h.matmul(ps[:, X * 512 + hh * 256:X * 512 + (hh + 1) * 256], lhsT=PTc[p][:, hh, mc, :], rhs=vc[:, mc, hh, 0:256], start=(mc == 0), stop=(mc == 1)),
                           reads=[b_PTc[p], b_vc], writes=[pbufs[X], pbufs[X + 1]])
                for mc in range(2):
                    pg.add('pe', lambda h, hh=hh, mc=mc: h.matmul(bank(T)[:, 256 + hh:256 + hh + 1], lhsT=PTc[p][:, hh, mc, :], rhs=vc[:, mc, hh, 256:257], start=(mc == 0), stop=(mc == 1)),
                           reads=[b_PTc[p], b_vc], writes=[pbufs[T]])

        def st9(t, p):
            X = 2 * p; T = 4 + p
            pg.add('dve', lambda h: h.reciprocal(out=rsi[p], in_=bank(T)[:, 256:260]), reads=[pbufs[T]], writes=[b_rsi[p]])
            pg.add('dve', lambda h: h.tensor_tensor(out=ocb[p].rearrange("p (h d) -> p h d", d=256), in0=bank(X, 2).rearrange("p (h d) -> p h d", d=256),
                                                    in1=rsi[p].unsqueeze(2).to_broadcast([P, 4, 256]), op=ALU.mult),
                   reads=[pbufs[X], pbufs[X + 1], b_rsi[p]], writes=[b_ocb[p]])

        def st10(t, p):
            transpose8(ocb[p], b_ocb[p], 4 + p, ocT[p], b_ocT[p])

        def st11(t, p):
            X = 2 * p
            for nb in range(2):
                for kc in range(8):
                    pg.add('pe', lambda h, nb=nb, kc=kc: h.matmul(bank(X + nb), lhsT=ocT[p][:, kc, :], rhs=wco[:, kc, nb * 512:(nb + 1) * 512], start=(kc == 0), stop=(kc == 7)),
                           reads=[b_ocT[p], b_wco], writes=[pbufs[X + nb]])

        def st12(t, p):
            X = 2 * p
            pg.add('dve', lambda h: h.tensor_tensor(out=x2[p], in0=bank(X, 2), in1=x1[p], op=ALU.add), reads=[pbufs[X], pbufs[X + 1], b_x1[p]], writes=[b_x2[p]])
            dma('pool', X2sc[sq_['tile0'] + t], x2[p], [b_x2[p]], [], b_x2[p])

        stages = [st1, st2, st3, st4, st5, st6, st7, st8, st9, st10, st11, st12]
        for t0 in range(0, NL, 2):
            pair = [(t0, 0)] + ([(t0 + 1, 1)] if t0 + 1 < NL else [])
            for (t, p) in pair:
                st_load(t, p)
            for f in stages:
                for (t, p) in pair:
                    f(t, p)
        pg.barrier()
        stop_at('P1')
        A.pop()
        A.pop()

    try:
        for si_, sqd in enumerate(seqs):
            do_seq(si_, sqd)
    except StopBuild:
        pg.emit(nc)
        return nc

    A.push()
    wup = A.alloc([P, 8, 4 * D], BF16); wdn = A.alloc([P, 32, D], BF16)
    b_wup = [Buf("wup%d" % i) for i in range(4)]; b_wdn = [Buf("wdn%d" % i) for i in range(4)]
    vup = w_up.rearrange("(kc p) n -> p kc n", p=P)
    vdn = w_down.rearrange("(fc p) n -> p fc n", p=P)
    for i in range(4):
        dma('pool', wup[:, :, i * 1024:(i + 1) * 1024], vup[:, :, i * 1024:(i + 1) * 1024], [], [b_wup[i]], b_wup[i])
    for i in range(4):
        dma('pool', wdn[:, i * 8:(i + 1) * 8, :], vdn[:, i * 8:(i + 1) * 8, :], [], [b_wdn[i]], b_wdn[i])
    TB = 2
    gmlp_b, b_gmlp = gain_tile(g_mlp, "gmlp")
    x2t = [A.alloc([P, D], F32) for _ in range(2 * TB)]; b_x2t = [Buf("x2t%d" % i) for i in range(2 * TB)]
    mj = A.alloc([P, D], BF16); b_mj = Buf("mj")
    ms = [A.alloc([P, 1], F32) for _ in range(2 * TB)]; b_ms = [Buf("ms%d" % i) for i in range(2 * TB)]
    xh2 = [A.alloc([P, D], BF16) for _ in range(2)]; b_xh2 = [Buf("xh2_%d" % i) for i in range(2)]
    xh2T = [A.alloc([P, 8, TB * P], BF16) for _ in range(2)]
    b_xh2T = [[Buf("xh2T%d_%d" % (i, k)) for k in range(TB)] for i in range(2)]
    uT = A.alloc([P, 32, TB * P], BF16); b_uT = [Buf("uT%d" % i) for i in range(32)]
    ur = [A.alloc([P, TB * P], F32) for _ in range(2)]; b_ur = [Buf("ur%d" % i) for i in range(2)]
    out_aps = []
    for sq_ in seqs:
        for t in range(sq_['NL']):
            out_aps.append(sq_['yout'][t])
    nb_ = NTL // TB
    assert NTL % TB == 0
    for b in range(nb_):
        bs = b % 2
        for k in range(TB):
            gt = b * TB + k
            sl = bs * TB + k
            dma('sp', x2t[sl], X2sc[gt], [], [b_x2t[sl]], b_x2t[sl])
            rmsnorm_tile(x2t[sl], gmlp_b, xh2[k % 2], mj, ms[sl], b_x2t[sl], b_xh2[k % 2], b_mj, b_ms[sl], b_gmlp)
            pv = bankb(6 + (k % 2))
            for kc in range(8):
                pg.add('pe', lambda h, kc=kc, k=k, pv=pv: h.transpose(out=pv[:, kc * 128:(kc + 1) * 128], in_=xh2[k % 2][:, kc * 128:(kc + 1) * 128], identity=identb),
                       reads=[b_xh2[k % 2], cb[0]], writes=[pbufs[6 + (k % 2)]])
            pg.add('act', lambda h, k=k, pv=pv, bs=bs: h.copy(out=xh2T[bs][:, :, k * P:(k + 1) * P], in_=pv.rearrange("p (a b) -> p a b", b=128)),
                   reads=[pbufs[6 + (k % 2)]], writes=[b_xh2T[bs][k]])
        for fc in range(32):
            pb = fc % 2
            for kc in range(8):
                pg.add('pe', lambda h, fc=fc, kc=kc, pb=pb, bs=bs: h.matmul(bank(pb)[:, 0:TB * P], lhsT=wup[:, kc, fc * 128:(fc + 1) * 128], rhs=xh2T[bs][:, kc, :],
                                                                            start=(kc == 0), stop=(kc == 7)),
                       reads=b_xh2T[bs] + [b_wup[fc // 8]], writes=[pbufs[pb]])
            pg.add('act', lambda h, fc=fc, pb=pb: h.activation(out=ur[pb], in_=bank(pb)[:, 0:TB * P], func=AF.Relu), reads=[pbufs[pb]], writes=[b_ur[pb]])
            eng = 'dve'
            pg.add(eng, lambda h, fc=fc, pb=pb: h.tensor_tensor(out=uT[:, fc, :], in0=ur[pb], in1=ur[pb], op=ALU.mult), reads=[b_ur[pb]], writes=[b_uT[fc]])
        for k in range(TB):
            gt = b * TB + k
            sl = bs * TB + k
            pb0 = 2 + 2 * (k % 2)
            for nb in range(2):
                for fc in range(32):
                    pg.add('pe', lambda h, nb=nb, fc=fc, k=k, pb0=pb0: h.matmul(bank(pb0 + nb), lhsT=uT[:, fc, k * P:(k + 1) * P], rhs=wdn[:, fc, nb * 512:(nb + 1) * 512],
                                                                                start=(fc == 0), stop=(fc == 31)),
                           reads=[b_uT[fc], b_wdn[fc // 8]], writes=[pbufs[pb0 + nb]])
            pg.add('dve', lambda h, k=k, sl=sl, pb0=pb0: h.tensor_tensor(out=x2t[sl], in0=bank(pb0, 2), in1=x2t[sl], op=ALU.add),
                   reads=[pbufs[pb0], pbufs[pb0 + 1], b_x2t[sl]], writes=[b_x2t[sl]])
            dma('pool', out_aps[gt], x2t[sl], [b_x2t[sl]], [], b_x2t[sl])
    A.pop()
    pg.emit(nc)
    return nc


def host_tables(cfg, core):
    bf = ml_dtypes.bfloat16
    N1P, N1S, LP = cfg.N1P, cfg.N1S, cfg.LP

    def rope_tab(S):
        n = np.arange(S)
        r = (n // 64).astype(np.float32); c = (n % 64).astype(np.float32)
        inv = (1.0 / (10000.0 ** (np.arange(0, 64, 2, dtype=np.float32) / 64.0))).astype(np.float32)
        ar = r[:, None] * inv[None, :]; ac = c[:, None] * inv[None, :]
        cr, sr, cc, sc = np.cos(ar), np.sin(ar), np.cos(ac), np.sin(ac)
        C = np.concatenate([cr, cr, cc, cc], axis=1)
        S_ = np.concatenate([-sr, sr, -sc, sc], axis=1)
        return np.concatenate([C, S_], axis=1).astype(np.float32)

    def dft_tab(S):
        n = np.arange(S, dtype=np.float64)[:, None]; k2 = np.arange(128, dtype=np.float64)[None, :]
        th = 2 * np.pi * ((n * k2) % S) / S
        fr = np.cos(th) / np.sqrt(S); fi = np.sin(th) / np.sqrt(S)
        return np.concatenate([fr, fi, -fi], axis=1).astype(np.float32).astype(bf)

    def w2_tab(N1, k1s):
        j = np.arange(N1, dtype=np.float64)[:, None]; k1 = np.asarray(k1s, dtype=np.float64)[None, :]
        th = 2 * np.pi * ((j * k1) % N1) / N1
        return np.concatenate([np.cos(th), -np.sin(th)], axis=1).astype(np.float32).astype(bf)

    ropeP = rope_tab(cfg.SP)
    c64 = np.arange(64, dtype=np.float64)
    th = 2 * np.pi * np.outer(c64, c64) / 64
    C64 = np.cos(th) / 8.0; S64 = np.sin(th) / 8.0
    Z = np.zeros((64, 64))
    bdc = np.block([[C64, Z], [Z, C64]]).astype(np.float32)
    bds = np.block([[S64, Z], [Z, S64]]).astype(np.float32)
    cst = np.zeros((128, 16), np.float32)
    cst[:, 0] = EPS; cst[:, 1] = -0.5; cst[:64, 2] = 1.0; cst[64:, 3] = 1.0; cst[:, 4] = 1.0
    return dict(
        ropeP=ropeP, ropePl=np.ascontiguousarray(ropeP[core * LP * 128:(core + 1) * LP * 128]), ropeS=rope_tab(cfg.SS),
        dftP=dft_tab(cfg.SP), dftS=dft_tab(cfg.SS),
        w2P=w2_tab(N1P, np.arange(core * LP, (core + 1) * LP)), w2S=w2_tab(N1S, np.arange(N1S)),
        identb=np.eye(128, dtype=np.float32).astype(bf), identf=np.eye(128, dtype=np.float32),
        bdc=bdc, bds=bds, cst=cst)


_CACHE = {}


def run(cfg, inputs):
    key = (cfg.N1P, cfg.N1S, cfg.NS)
    if key not in _CACHE:
        _CACHE[key] = build_program(cfg)
    nc = _CACHE[key]
    f = lambda a: np.ascontiguousarray(np.asarray(a, dtype=np.float32))
    xp = f(inputs["x_prompt"])[0]; xs = f(inputs["x_sample"])
    memp = f(inputs["mem_prompt"])[0]; mems = f(inputs["mem_sample"])
    LP, NS = cfg.LP, cfg.NS
    common = dict(
        xp=xp, memp=memp,
        g_mix=f(inputs["g_mix"])[0], g_cross=f(inputs["g_cross"])[0], g_mem=f(inputs["g_mem"])[0], g_mlp=f(inputs["g_mlp"])[0],
        g_q=f(inputs["g_q"])[0], g_k=f(inputs["g_k"])[0], g_cq=f(inputs["g_cq"])[0], g_ck=f(inputs["g_ck"])[0],
        w_in=f(inputs["w_in"])[0], w_f=f(inputs["w_fourier"])[0].reshape(256, 64),
        w_out=f(inputs["w_out"])[0], w_cq=f(inputs["w_cq"])[0], w_ckv=f(inputs["w_ckv"])[0], w_co=f(inputs["w_co"])[0],
        w_up=f(inputs["w_up"])[0], w_down=f(inputs["w_down"])[0])
    in_maps = []
    for c in range(NCORES):
        m = dict(common)
        m["xpl"] = np.ascontiguousarray(xp[c * LP * 128:(c + 1) * LP * 128])
        m["xs"] = np.ascontiguousarray(xs[c * NS:(c + 1) * NS])
        m["mems"] = np.ascontiguousarray(mems[c * NS:(c + 1) * NS])
        m.update(host_tables(cfg, c))
        in_maps.append(m)
    res = run_bass_kernel_spmd(nc, in_maps, core_ids=list(range(NCORES)))
    yp = np.concatenate([res.results[c]["yp"] for c in range(NCORES)], axis=0)[None]
    ys = np.concatenate([res.results[c]["ys"] for c in range(NCORES)], axis=0)
    return (yp.astype(np.float32), ys.astype(np.float32))


def kernel(**inputs):
    return run(Cfg(128, 16, 2), inputs)
```

```python
import numpy as np
import ml_dtypes
from contextlib import ExitStack
import concourse.bass as bass
import concourse.mybir as mybir
from concourse.bass_utils import run_bass_kernel_spmd

F32 = mybir.dt.float32
BF16 = mybir.dt.bfloat16
U8 = mybir.dt.uint8
AF = mybir.ActivationFunctionType
ALU = mybir.AluOpType
AX = mybir.AxisListType

P = 128
D = 1024
VW = 136
NCORES = 8
EPS = 1e-6
ENGS = ['pe', 'act', 'dve', 'pool', 'sp']


class Buf:
    def __init__(self, name):
        self.name = name
        self.last_w = None
        self.readers = []
        self.sem = None
        self.dcount = 0
        self.last_dma = None


class Op:
    pass


class StopBuild(Exception):
    pass


import os
KSTOP = os.environ.get('KSTOP', '')
KC_STEP = float(os.environ.get('KC_STEP', '99'))
KC_TILES = int(os.environ.get('KC_TILES', '9999'))
KC_NR = int(os.environ.get('KC_NR', '99'))


def stop_at(name):
    if KSTOP == name:
        raise StopBuild()


class Prog:
    def __init__(self):
        self.ops = {e: [] for e in ENGS}
        self.bar = {e: [] for e in ENGS}
        self.epoch = 0
        self.epoch_used = {'sw': 0, 'hw': 0}
        self.kind_slots = {'sw': [], 'hw': []}
        self.slot_count = []
        self.slot_last = []

    def add(self, eng, fn, reads=(), writes=(), dma=None):
        op = Op()
        op.eng = eng
        op.fn = fn
        op.deps = []
        op.signal = False
        op.dma = dma
        op.idx = len(self.ops[eng])
        op.sig = 0
        op.raw = set()
        for b in reads:
            if b.last_w is not None:
                op.deps.append(b.last_w)
                op.raw.add(id(b.last_w))
        for b in writes:
            if b.last_w is not None:
                op.deps.append(b.last_w)
            op.deps.extend(b.readers)
        for b in reads:
            b.readers.append(op)
        for b in writes:
            b.last_w = op
            b.readers = []
        if dma is not None:
            kind = 'sw' if eng == 'pool' else 'hw'
            key = (self.epoch, kind)
            if not hasattr(dma, 'slots'):
                dma.slots = {}
            if key not in dma.slots:
                pool = self.kind_slots[kind]
                u = self.epoch_used[kind]
                self.epoch_used[kind] = u + 1
                if u == len(pool):
                    pool.append(len(self.slot_count))
                    self.slot_count.append(0)
                    self.slot_last.append(None)
                dma.slots[key] = pool[u]
            sl = dma.slots[key]
            self.slot_count[sl] += 16
            op.slot = sl
            op.dval = self.slot_count[sl]
            self.slot_last[sl] = op
        if self.bar[eng]:
            op.deps.extend(self.bar[eng])
            self.bar[eng] = []
        self.ops[eng].append(op)
        return op

    def barrier(self):
        deps = []
        for e in ENGS:
            for o in reversed(self.ops[e]):
                if o.dma is None:
                    deps.append(o)
                    break
        for kind in ('sw', 'hw'):
            for u in range(self.epoch_used[kind]):
                deps.append(self.slot_last[self.kind_slots[kind][u]])
        for e in ENGS:
            self.bar[e] = self.bar[e] + deps
        self.epoch += 1
        self.epoch_used = {'sw': 0, 'hw': 0}

    def emit(self, nc):
        ksim = bool(os.environ.get('KSIM'))
        for e in ENGS:
            for o in self.ops[e]:
                o.waits = []
                best = {}
                bestd = {}
                same = None
                for d in o.deps:
                    if d is o:
                        continue
                    if d.dma is not None:
                        if d.slot not in bestd or bestd[d.slot].dval < d.dval:
                            bestd[d.slot] = d
                    elif d.eng == o.eng:
                        if e == 'pe':
                            continue
                        if e in ('pool', 'act') and not ksim:
                            continue
                        if (id(d) in o.raw or ksim) and (same is None or same.idx < d.idx):
                            same = d
                    else:
                        if d.eng not in best or best[d.eng].idx < d.idx:
                            best[d.eng] = d
                for d in bestd.values():
                    o.waits.append(('d', d))
                for d in best.values():
                    d.signal = True
                    o.waits.append(('c', d))
                if same is not None and (o.idx - same.idx <= 2 or ksim):
                    same.signal = True
                    o.waits.append(('c', same))
        for e in ENGS:
            c = 0
            for o in self.ops[e]:
                if o.dma is None and o.signal:
                    c += 1
                    o.sig = c
        with ExitStack() as es:
            csem = {e: es.enter_context(nc.semaphore("c_" + e)) for e in ENGS}
            dsem = [es.enter_context(nc.semaphore("d%d" % i)) for i in range(len(self.slot_count))]
            block = es.enter_context(nc.Block())
            self.n_wait = 0

            def run(e, h):
                waited = {}
                for o in self.ops[e]:
                    for kind, d in o.waits:
                        if kind == 'd':
                            s, v, k = dsem[d.slot], d.dval, ('d', d.slot)
                        else:
                            s, v, k = csem[d.eng], d.sig, ('c', d.eng)
                        if waited.get(k, 0) >= v:
                            continue
                        waited[k] = v
                        h.wait_ge(s, v)
                        self.n_wait += 1
                    ins = o.fn(h)
                    if o.dma is not None:
                        ins.then_inc(dsem[o.slot], 16)
                    elif o.signal:
                        ins.then_inc(csem[e], 1)
                if e == 'sp':
                    for i, sm in enumerate(dsem):
                        h.wait_ge(sm, self.slot_count[i])

            @block.tensor
            def _(h):
                run('pe', h)

            @block.scalar
            def _(h):
                run('act', h)

            @block.vector
            def _(h):
                run('dve', h)

            @block.gpsimd
            def _(h):
                run('pool', h)

            @block.sync
            def _(h):
                run('sp', h)
        print("ops:", {e: len(v) for e, v in self.ops.items()}, "waits:", self.n_wait, "dma sems:", len(self.slot_count), flush=True)


class Arena:
    def __init__(self, ap_u8, size):
        self.ar = ap_u8
        self.size = size
        self.off = 0
        self.marks = []

    def push(self):
        self.marks.append(self.off)

    def pop(self):
        self.off = self.marks.pop()

    def alloc(self, shape, dtype):
        esz = 4 if dtype == F32 else 2
        n = 1
        for s in shape[1:]:
            n *= s
        nb = (n * esz + 63) // 64 * 64
        assert self.off + nb <= self.size, ("SBUF arena overflow", self.off, nb, self.size)
        v = self.ar[:, self.off:self.off + n * esz].bitcast(dtype)
        self.off += nb
        if len(shape) == 3:
            v = v.rearrange("p (a b) -> p a b", b=shape[2])
        elif len(shape) == 4:
            v = v.rearrange("p (a b c) -> p a b c", b=shape[2], c=shape[3])
        return v


class Cfg:
    def __init__(self, n1p=128, n1s=16, ns=2):
        self.N1P = n1p
        self.N1S = n1s
        self.NS = ns
        self.LP = n1p // NCORES
        self.SP = 128 * n1p
        self.SS = 128 * n1s
        self.NM = 256


def build_program(cfg, debug=False):
    nc = bass.Bass("TRN2", target_bir_lowering=False)
    N1P, N1S, NS, LP = cfg.N1P, cfg.N1S, cfg.NS, cfg.LP
    SP_, SS_ = cfg.SP, cfg.SS
    NTL = LP + NS * N1S

    def din(name, shape, dt=F32):
        return nc.dram_tensor(name, list(shape), dt, kind="ExternalInput").ap()

    def dscr(name, shape, dt):
        return nc.dram_tensor(name, list(shape), dt, kind="Internal").ap()

    xp = din("xp", [SP_, D])
    xpl = din("xpl", [LP * P, D])
    xs = din("xs", [NS, SS_, D])
    memp = din("memp", [256, D])
    mems = din("mems", [NS, 256, D])
    g_mix = din("g_mix", [D]); g_cross = din("g_cross", [D]); g_mem = din("g_mem", [D]); g_mlp = din("g_mlp", [D])
    g_q = din("g_q", [128]); g_k = din("g_k", [128]); g_cq = din("g_cq", [256]); g_ck = din("g_ck", [256])
    w_in = din("w_in", [D, 1536]); w_f = din("w_f", [256, 64])
    w_out = din("w_out", [D, D]); w_cq = din("w_cq", [D, D]); w_ckv = din("w_ckv", [D, 2 * D]); w_co = din("w_co", [D, D])
    w_up = din("w_up", [D, 4 * D]); w_down = din("w_down", [4 * D, D])
    ropeP = din("ropeP", [SP_, 256]); ropePl = din("ropePl", [LP * P, 256]); ropeS = din("ropeS", [SS_, 256])
    dftP = din("dftP", [SP_, 384], BF16); dftS = din("dftS", [SS_, 384], BF16)
    w2P = din("w2P", [N1P, 2 * LP], BF16); w2S = din("w2S", [N1S, 2 * N1S], BF16)
    identb_d = din("identb", [P, P], BF16); identf_d = din("identf", [P, P])
    bdc_d = din("bdc", [P, P]); bds_d = din("bds", [P, P]); cst_d = din("cst", [P, 16])

    yp = nc.dram_tensor("yp", [LP * P, D], F32, kind="ExternalOutput").ap()
    ys = nc.dram_tensor("ys", [NS, SS_, D], F32, kind="ExternalOutput").ap()

    N1MAX = max(N1P, N1S)
    WFsc = dscr("WFsc", [D, 512], BF16)
    KVsc = dscr("KVsc", [P, N1MAX, 512], BF16)
    Tsc = dscr("Tsc", [P, N1MAX, 512], BF16)
    X2sc = dscr("X2sc", [NTL, P, D], F32)

    ARENA = 206 * 1024
    with nc.sbuf_tensor("arena", [P, ARENA], U8) as ar_t:
        ar_ap = ar_t[:]
    with nc.psum_tensor("psum", [P, 4096], F32) as ps_t:
        ps = ps_t[:]
    A = Arena(ar_ap, ARENA)
    pg = Prog()

    def bank(b, n=1):
        return ps[:, b * 512:(b + n) * 512]

    def bankb(b):
        return ps[:, b * 512:(b + 1) * 512].bitcast(BF16)

    pbufs = [Buf("psb%d" % i) for i in range(8)]

    identb = A.alloc([P, P], BF16); identf = A.alloc([P, P], F32)
    cst = A.alloc([P, 16], F32)
    gk_b = A.alloc([P, 2, 128], F32); gq_b = A.alloc([P, 6, 128], F32)
    gcq_b = A.alloc([P, 4, 256], F32); gck_b = A.alloc([P, 4, 256], F32)
    ones_b = A.alloc([P, 8], BF16)
    bconst = Buf("const")

    def dma(eng, out, in_, reads, writes, sembuf):
        return pg.add(eng, lambda h, o=out, i=in_: h.dma_start(out=o, in_=i), reads=reads, writes=writes, dma=sembuf)

    cb = [Buf("cb%d" % i) for i in range(16)]
    dma('sp', identb, identb_d, [], [cb[0]], cb[0])
    dma('sp', identf, identf_d, [], [cb[1]], cb[1])
    dma('sp', cst, cst_d, [], [cb[2]], cb[2])
    def gain_tile(g_ap, name):
        t = A.alloc([P, D], F32)
        b = Buf(name)
        dma('sp', t, g_ap.partition_broadcast(P), [], [b], b)
        return t, b
    for h in range(2):
        dma('sp', gk_b[:, h, :], g_k.partition_broadcast(P), [], [cb[7]], cb[7])
    for h in range(6):
        dma('sp', gq_b[:, h, :], g_q.partition_broadcast(P), [], [cb[8]], cb[8])
    for h in range(4):
        dma('sp', gcq_b[:, h, :], g_cq.partition_broadcast(P), [], [cb[9]], cb[9])
        dma('sp', gck_b[:, h, :], g_ck.partition_broadcast(P), [], [cb[10]], cb[10])
    pg.add('dve', lambda h: h.memset(ones_b, 1.0), [], [cb[11]])
    CONSTS = cb[:12]
    eps_c = cst[:, 0:1]; mhalf_c = cst[:, 1:2]

    def rstd_ops(ss, n, bufs_r, bufs_w):
        pg.add('pool', lambda h: h.tensor_tensor(out=ss, in0=ss, in1=eps_c.to_broadcast([P, n]), op=ALU.add),
               reads=[cb[2]] + bufs_r, writes=bufs_w)
        pg.add('pool', lambda h: h.tensor_tensor(out=ss, in0=ss, in1=mhalf_c.to_broadcast([P, n]), op=(ALU.add if os.environ.get('KC_X') == '4' else ALU.pow)),
               reads=[cb[2]] + bufs_r, writes=bufs_w)

    A.push()
    bdc = A.alloc([P, P], F32); bds = A.alloc([P, P], F32)
    wfs = A.alloc([P, 2, 64], F32)
    winf = A.alloc([P, 8, 256], F32)
    winfT = A.alloc([P, 2, D], F32)
    rhsq = A.alloc([P, 2, 512], F32)
    wfo = A.alloc([P, 8, 512], BF16)
    b_bd = Buf("bd"); b_wfs = Buf("wfs"); b_winf = Buf("winf"); b_winfT = Buf("winfT"); b_rhsq = Buf("rhsq"); b_wfo = Buf("wfo")
    b_bd2 = Buf("bd2")
    dma('sp', bdc, bdc_d, [], [b_bd], b_bd)
    dma('sp', bds, bds_d, [], [b_bd2], b_bd2)
    dma('sp', wfs, w_f.rearrange("(q p) e -> p q e", p=P), [], [b_wfs], b_wfs)
    dma('sp', winf, w_in.rearrange("(kc p) n -> p kc n", p=P)[:, :, 0:256], [], [b_winf], b_winf)
    pg.add('dve', lambda h: h.memset(rhsq, 0.0), [], [b_rhsq])
    for q in range(2):
        pg.add('pe', lambda h, q=q: h.matmul(bank(0)[:, q * 128:q * 128 + 64], lhsT=bdc, rhs=wfs[:, q, :], start=True, stop=True),
               reads=[b_bd, b_wfs], writes=[pbufs[0]])
        pg.add('pe', lambda h, q=q: h.matmul(bank(0)[:, q * 128 + 64:q * 128 + 128], lhsT=bds, rhs=wfs[:, q, :], start=True, stop=True),
               reads=[b_bd2, b_wfs], writes=[pbufs[0]])
    for q in range(2):
        for gl in range(2):
            g = 2 * q + gl
            msk = cst[:, 2 + gl:3 + gl]
            for ri in range(2):
                pg.add('dve', lambda h, q=q, g=g, ri=ri, msk=msk: h.tensor_scalar(
                    out=rhsq[:, q, ri * 256 + g * 64: ri * 256 + g * 64 + 64],
                    in0=bank(0)[:, q * 128 + ri * 64: q * 128 + ri * 64 + 64],
                    scalar1=msk, scalar2=None, op0=ALU.mult), reads=[pbufs[0], cb[2]], writes=[b_rhsq])
    for kc in range(8):
        for q in range(2):
            pg.add('pe', lambda h, kc=kc, q=q: h.transpose(out=bank(1 + q)[:, 0:128], in_=winf[:, kc, q * 128:(q + 1) * 128], identity=identf),
                   reads=[b_winf, cb[1]], writes=[pbufs[1 + q]])
            pg.add('act', lambda h, kc=kc, q=q: h.copy(out=winfT[:, q, kc * 128:(kc + 1) * 128], in_=bank(1 + q)[:, 0:128]),
                   reads=[pbufs[1 + q]], writes=[b_winfT])
    for kc in range(8):
        pb = 3 + (kc % 2)
        for q in range(2):
            pg.add('pe', lambda h, kc=kc, q=q, pb=pb: h.matmul(bank(pb), lhsT=winfT[:, q, kc * 128:(kc + 1) * 128], rhs=rhsq[:, q, :],
                                                           start=(q == 0), stop=(q == 1)),
                   reads=[b_winfT, b_rhsq], writes=[pbufs[pb]])
        pg.add('act', lambda h, kc=kc, pb=pb: h.copy(out=wfo[:, kc, :], in_=bank(pb)), reads=[pbufs[pb]], writes=[b_wfo])
    dma('pool', WFsc.rearrange("(kc p) n -> p kc n", p=P), wfo, [b_wfo], [], b_wfo)
    pg.barrier()
    if KSTOP == 'W0':
        pg.emit(nc)
        return nc
    A.pop()

    seqs = []
    seqs.append(dict(kind='P', N1=N1P, NL=LP, xctx=xp.rearrange("(p j) d -> j p d", j=N1P),
                     xloc=xpl.rearrange("(i p) d -> i p d", p=P),
                     rope_ctx=ropeP.rearrange("(p j) d -> j p d", j=N1P),
                     rope_loc=ropePl.rearrange("(i p) d -> i p d", p=P),
                     dft=dftP.rearrange("(p j) d -> j p d", j=N1P), w2=w2P, NK1=LP,
                     mem=memp, yout=yp.rearrange("(i p) d -> i p d", p=P), tile0=0))
    for s in range(NS):
        seqs.append(dict(kind='S', N1=N1S, NL=N1S, xctx=xs[s].rearrange("(p j) d -> j p d", j=N1S),
                         xloc=xs[s].rearrange("(p j) d -> j p d", j=N1S),
                         rope_ctx=ropeS.rearrange("(p j) d -> j p d", j=N1S),
                         rope_loc=ropeS.rearrange("(p j) d -> j p d", j=N1S),
                         dft=dftS.rearrange("(p j) d -> j p d", j=N1S), w2=w2S, NK1=N1S,
                         mem=mems[s], yout=ys[s].rearrange("(p j) d -> j p d", j=N1S), tile0=LP + s * N1S))

    INV32 = 1.0 / 32.0

    def load_w_bf16(dst, src_ap, buf, nsplit=1):
        kcs = dst.shape[1]
        v = src_ap.rearrange("(kc p) n -> p kc n", p=P)
        step = kcs // nsplit
        for i in range(nsplit):
            dma('pool', dst[:, i * step:(i + 1) * step, :], v[:, i * step:(i + 1) * step, :], [], [buf], buf)

    def rmsnorm_tile(xt, gb, xh, junk, ss, b_x, b_xh, b_junk, b_ss, gbuf):
        pg.add('act', lambda h: h.activation(out=junk, in_=xt, func=AF.Square, scale=INV32, accum_out=ss),
               reads=[b_x], writes=[b_junk, b_ss])
        rstd_ops(ss, 1, [b_ss], [b_ss])
        pg.add('dve', lambda h: h.scalar_tensor_tensor(out=xh, in0=xt, scalar=ss, in1=gb, op0=ALU.mult, op1=ALU.mult),
               reads=[b_x, b_ss, gbuf], writes=[b_xh])

    def transpose8(src, b_src, pbank, dstT, b_dst, n=8, eng='act'):
        pv = bankb(pbank)
        for kc in range(n):
            pg.add('pe', lambda h, kc=kc: h.transpose(out=pv[:, kc * 128:(kc + 1) * 128], in_=src[:, kc * 128:(kc + 1) * 128], identity=identb),
                   reads=[b_src, cb[0]], writes=[pbufs[pbank]])
        if eng == 'act':
            pg.add('act', lambda h: h.copy(out=dstT, in_=pv[:, 0:n * 128].rearrange("p (a b) -> p a b", b=128)), reads=[pbufs[pbank]], writes=[b_dst])
        else:
            pg.add('dve', lambda h: h.tensor_copy(out=dstT, in_=pv[:, 0:n * 128].rearrange("p (a b) -> p a b", b=128)), reads=[pbufs[pbank]], writes=[b_dst])

    def normrope(zsrc, b_z, H, gain_b, gbuf, rt, b_rt, tmp, b_tmp, kr, b_kr):
        sq, kn, t1, t2, ssq = tmp
        z3 = zsrc.rearrange("p (h d) -> p h d", d=128)
        KX = os.environ.get('KC_X', '')
        if KX == '2':
            pg.add('act', lambda h: h.activation(out=sq[:, 0:H, :], in_=z3, func=AF.Square, scale=float(1.0 / np.sqrt(128.0))),
                   reads=list(b_z), writes=[b_tmp[0], b_tmp[4]])
        else:
            for hh in range(H):
                pg.add('act', lambda h, hh=hh: h.activation(out=sq[:, hh, :], in_=z3[:, hh, :], func=AF.Square, scale=float(1.0 / np.sqrt(128.0)), accum_out=ssq[:, hh, 0:1]),
                       reads=list(b_z), writes=[b_tmp[0], b_tmp[4]])
        if KX != '1':
            for hh in range(H):
                rstd_ops(ssq[:, hh, 0:1], 1, [b_tmp[4]], [b_tmp[4]])
        if KC_NR < 3:
            return
        pg.add('dve', lambda h: h.tensor_tensor(out=kn[:, 0:H, :], in0=z3, in1=gain_b, op=ALU.mult),
               reads=list(b_z) + [gbuf, b_tmp[0]], writes=[b_tmp[1]])
        pg.add('dve', lambda h: h.tensor_tensor(out=kn[:, 0:H, :], in0=kn[:, 0:H, :], in1=ssq[:, 0:H, 0:1].to_broadcast([P, H, 128]), op=ALU.mult),
               reads=[b_tmp[4], b_tmp[1]], writes=[b_tmp[1]])
        if KC_NR < 5:
            return
        C_b = rt[:, 0:128].unsqueeze(1).to_broadcast([P, H, 128])
        pg.add('dve', lambda h: h.tensor_tensor(out=t1[:, 0:H, :], in0=kn[:, 0:H, :], in1=C_b, op=ALU.mult),
               reads=[b_tmp[1], b_rt], writes=[b_tmp[2]])
        kn5 = kn[:, 0:H, :].rearrange("p h (a t k) -> p h a t k", a=2, t=2)
        t25 = t2[:, 0:H, :].rearrange("p h (a t k) -> p h a t k", a=2, t=2)
        S5 = rt[:, 128:256].rearrange("p (a t k) -> p a t k", a=2, t=2)
        for a in range(2):
            for t in range(2):
                pg.add('dve', lambda h, a=a, t=t: h.tensor_tensor(out=t25[:, :, a, t, :], in0=kn5[:, :, a, 1 - t, :],
                                                                 in1=S5[:, a, t, :].unsqueeze(1).to_broadcast([P, H, 32]), op=ALU.mult),
                       reads=[b_tmp[1], b_rt], writes=[b_tmp[3]])
        pg.add('dve', lambda h: h.tensor_tensor(out=kr[:, 0:H, :], in0=t1[:, 0:H, :], in1=t2[:, 0:H, :], op=ALU.add),
               reads=[b_tmp[2], b_tmp[3]], writes=[b_kr])

    def do_seq(si, sq_):
        N1 = sq_['N1']; NL = sq_['NL']; kind = sq_['kind']
        NLOC = NL * P
        A.push()
        QT = A.alloc([P, 6, NLOC], BF16)
        foT = A.alloc([P, 2, NLOC], BF16)
        QB = min(512, NLOC)
        NQB = NLOC // QB
        b_QT = [[Buf("QT%d_%d" % (h, qb)) for qb in range(NQB)] for h in range(6)]
        b_foT = Buf("foT")

        A.push()
        Wctx = A.alloc([P, 8, 1024], BF16)
        Wq = A.alloc([P, 8, 768], BF16)
        b_Wctx = Buf("Wctx"); b_Wq = Buf("Wq"); b_Wctx2 = Buf("Wctx2")
        dma('sp', Wctx[:, :, 0:512], WFsc.rearrange("(kc p) n -> p kc n", p=P), [], [b_Wctx], b_Wctx)
        dma('pool', Wctx[:, :, 512:1024], w_in.rearrange("(kc p) n -> p kc n", p=P)[:, :, 1024:1536], [], [b_Wctx2], b_Wctx2)
        dma('pool', Wq, w_in.rearrange("(kc p) n -> p kc n", p=P)[:, :, 256:1024], [], [b_Wq], b_Wq)
        NSL = 2
        gmix_b, b_gmix = gain_tile(g_mix, "gmix")
        xt = [A.alloc([P, D], F32) for _ in range(3)]; b_xt = [Buf("xt%d" % i) for i in range(3)]
        rt = [A.alloc([P, 256], F32) for _ in range(NSL)]; b_rt = [Buf("rt%d" % i) for i in range(NSL)]
        ft = [A.alloc([P, 384], BF16) for _ in range(NSL)]; b_ft = [Buf("ft%d" % i) for i in range(NSL)]
        junk = A.alloc([P, D], BF16); b_junk = Buf("junk")
        ss = [A.alloc([P, 1], F32) for _ in range(NSL)]; b_ss = [Buf("ss%d" % i) for i in range(NSL)]
        xh = [A.alloc([P, D], BF16) for _ in range(NSL)]; b_xh = [Buf("xh%d" % i) for i in range(NSL)]
        xhT = [A.alloc([P, 8, 128], BF16) for _ in range(NSL)]; b_xhT = [Buf("xhT%d" % i) for i in range(NSL)]
        Yt = [A.alloc([P, 512], BF16) for _ in range(NSL)]; b_Yt = [Buf("Yt%d" % i) for i in range(NSL)]
        KVt = [A.alloc([P, 512], BF16) for _ in range(NSL)]; b_KVt = [Buf("KVt%d" % i) for i in range(NSL)]
        Tt = [A.alloc([P, 512], BF16) for _ in range(NSL)]; b_Tt = [Buf("Tt%d" % i) for i in range(NSL)]
        tmpk = (A.alloc([P, 2, 128], F32), A.alloc([P, 2, 128], F32), A.alloc([P, 2, 128], F32), A.alloc([P, 2, 128], F32), A.alloc([P, 8, 16], F32))
        b_tmpk = [Buf("tmpk%d" % i) for i in range(5)]
        tmpq = (A.alloc([P, 6, 128], F32), A.alloc([P, 6, 128], F32), A.alloc([P, 6, 128], F32), A.alloc([P, 6, 128], F32), A.alloc([P, 8, 16], F32))
        b_tmpq = [Buf("tmpq%d" % i) for i in range(5)]
        krk = A.alloc([P, 2, 128], BF16); b_krk = Buf("krk")
        krq = A.alloc([P, 6, 128], BF16); b_krq = Buf("krq")

        rt3 = [A.alloc([P, 256], F32) for _ in range(3)]; b_rt3 = [Buf("rt3_%d" % i) for i in range(3)]
        ft3 = [A.alloc([P, 384], BF16) for _ in range(3)]; b_ft3 = [Buf("ft3_%d" % i) for i in range(3)]

        def ctx_loads(j, it, do_ctx, do_q, xsrc, ropesrc):
            s3 = it % 3
            dma('sp', xt[s3], xsrc[j], [], [b_xt[s3]], b_xt[s3])
            dma('sp', rt3[s3], ropesrc[j], [], [b_rt3[s3]], b_rt3[s3])
            if do_ctx:
                dma('sp', ft3[s3], sq_['dft'][j], [], [b_ft3[s3]], b_ft3[s3])

        def ctx_part1(j, it, do_ctx, do_q, xsrc, ropesrc):
            s3 = it % 3; s2 = it % NSL
            if do_ctx and do_q:
                zc, zq = 0, 2
            elif do_ctx:
                zc, zq = 2 * (it % 2), None
            else:
                zc, zq = None, 2 * (it % 2)
            rmsnorm_tile(xt[s3], gmix_b, xh[s2], junk, ss[s2], b_xt[s3], b_xh[s2], b_junk, b_ss[s2], b_gmix)
            transpose8(xh[s2], b_xh[s2], 4 + s2, xhT[s2], b_xhT[s2])
            if do_ctx:
                for nb in range(2):
                    for kc in range(8):
                        pg.add('pe', lambda h, nb=nb, kc=kc: h.matmul(bank(zc + nb), lhsT=xhT[s2][:, kc, :], rhs=Wctx[:, kc, nb * 512:(nb + 1) * 512],
                                                                      start=(kc == 0), stop=(kc == 7)),
                               reads=[b_xhT[s2], b_Wctx, b_Wctx2], writes=[pbufs[zc + nb]])
            if do_q:
                for nb in range(2):
                    w = 512 if nb == 0 else 256
                    for kc in range(8):
                        pg.add('pe', lambda h, nb=nb, kc=kc, w=w: h.matmul(bank(zq + nb)[:, 0:w], lhsT=xhT[s2][:, kc, :], rhs=Wq[:, kc, nb * 512:nb * 512 + w],
                                                                           start=(kc == 0), stop=(kc == 7)),
                               reads=[b_xhT[s2], b_Wq], writes=[pbufs[zq + nb]])
            return zc, zq

        def ctx_part2(j, it, do_ctx, do_q, xsrc, ropesrc, zc, zq):
            s3 = it % 3; s2 = it % NSL
            rt_ = rt3[s3]; b_rt_ = b_rt3[s3]
            if do_ctx:
                pg.add('act', lambda h: h.copy(out=Yt[s2], in_=bank(zc)), reads=[pbufs[zc]], writes=[b_Yt[s2]])
                pg.add('act', lambda h: h.copy(out=KVt[s2][:, 256:512], in_=bank(zc + 1)[:, 256:512]),
                       reads=[pbufs[zc + 1]], writes=[b_KVt[s2]])
                normrope(bank(zc + 1)[:, 0:256], [pbufs[zc + 1]], 2, gk_b, cb[7], rt_, b_rt_, tmpk, b_tmpk, krk, b_krk)
                pv = bankb(7)
                for hh in range(2):
                    pg.add('pe', lambda h, hh=hh: h.transpose(out=pv[:, hh * 128:(hh + 1) * 128], in_=krk[:, hh, :], identity=identb),
                           reads=[b_krk, cb[0]], writes=[pbufs[7]])
                pg.add('act', lambda h: h.copy(out=KVt[s2][:, 0:256], in_=pv[:, 0:256]), reads=[pbufs[7]], writes=[b_KVt[s2]])
                dma('sp', KVsc[:, j, :], KVt[s2], [b_KVt[s2]], [], b_KVt[s2])
                Fr = ft3[s3][:, 0:128]; Fi = ft3[s3][:, 128:256]; nFi = ft3[s3][:, 256:384]
                mm = [(0, Fr, 0, True, False), (0, nFi, 256, False, True), (256, Fi, 0, True, False), (256, Fr, 256, False, True)]
                for (oc, lw, ic, st, sp_) in mm:
                    pg.add('pe', lambda h, oc=oc, lw=lw, ic=ic, st=st, sp_=sp_: h.matmul(bank(6)[:, oc:oc + 256], lhsT=lw, rhs=Yt[s2][:, ic:ic + 256], start=st, stop=sp_),
                           reads=[b_ft3[s3], b_Yt[s2]], writes=[pbufs[6]])
                pg.add('act', lambda h: h.copy(out=Tt[s2], in_=bank(6)), reads=[pbufs[6]], writes=[b_Tt[s2]])
                dma('sp', Tsc[:, j, :], Tt[s2], [b_Tt[s2]], [], b_Tt[s2])
            if do_q:
                normrope(bank(zq, 2)[:, 0:768], [pbufs[zq], pbufs[zq + 1]], 6, gq_b, cb[8], rt_, b_rt_, tmpq, b_tmpq, krq, b_krq)
                pv = bankb(7)
                qb_i = (j * P) // QB
                for hh in range(6):
                    if hh % 4 == 0:
                        cnt = min(4, 6 - hh)
                    pg.add('pe', lambda h, hh=hh: h.transpose(out=pv[:, (hh % 4) * 128:(hh % 4 + 1) * 128], in_=krq[:, hh, :], identity=identb),
                           reads=[b_krq, cb[0]], writes=[pbufs[7]])
                    if hh % 4 == cnt - 1:
                        h0 = hh - (cnt - 1)
                        pg.add('act', lambda h, h0=h0, cnt=cnt: h.copy(out=QT[:, h0:h0 + cnt, j * P:(j + 1) * P],
                                                                       in_=pv[:, 0:cnt * 128].rearrange("p (a b) -> p a b", b=128)),
                               reads=[pbufs[7]], writes=[b_QT[hq][qb_i] for hq in range(h0, h0 + cnt)])

        tiles = []
        if kind == 'P':
            for j in range(min(N1, KC_TILES)):
                tiles.append((j, len(tiles), True, False, sq_['xctx'], sq_['rope_ctx']))
            for i in range(min(NL, KC_TILES)):
                tiles.append((i, len(tiles), False, True, sq_['xloc'], sq_['rope_loc']))
        else:
            for j in range(N1):
                tiles.append((j, len(tiles), True, False, sq_['xctx'], sq_['rope_ctx']))
                tiles.append((j, len(tiles), False, True, sq_['xctx'], sq_['rope_ctx']))
        nt = len(tiles)
        if True:
            ctx_loads(*tiles[0])
            if nt > 1:
                ctx_loads(*tiles[1])
            zz = {0: ctx_part1(*tiles[0])}
            for ti in range(nt):
                if ti + 2 < nt:
                    ctx_loads(*tiles[ti + 2])
                if ti + 1 < nt:
                    zz[ti + 1] = ctx_part1(*tiles[ti + 1])
                ctx_part2(*tiles[ti], *zz[ti])
        else:
            ctx_loads(*tiles[0])
            for ti in range(nt):
                if ti + 1 < nt:
                    ctx_loads(*tiles[ti + 1])
                z_ = ctx_part1(*tiles[ti])
                ctx_part2(*tiles[ti], *z_)
        pg.barrier()
        stop_at('C')
        A.pop()

        A.push()
        NK1 = sq_['NK1']
        w2t = A.alloc([P, 2 * NK1], BF16); b_w2 = Buf("w2")
        dma('sp', w2t[0:N1, :], sq_['w2'], [], [b_w2], b_w2)
        G8 = 8
        tl = [A.alloc([P, G8, 512], BF16) for _ in range(3)]; b_tl = [Buf("tl%d" % i) for i in range(3)]
        K2G = min(128, 512 // NK1)
        for g8 in range(128 // G8):
            s = g8 % 3
            dma('sp', tl[s][0:N1], Tsc[g8 * G8:(g8 + 1) * G8, 0:N1, :].rearrange("k j c -> j k c"), [], [b_tl[s]], b_tl[s])
            for kk in range(G8):
                k2 = g8 * G8 + kk
                grp = k2 // K2G; pos = k2 % K2G
                for ch in range(2):
                    pb = 2 * (grp % 2) + ch
                    o = bank(pb)[:, pos * NK1:(pos + 1) * NK1]
                    pg.add('pe', lambda h, o=o, s=s, kk=kk, ch=ch: h.matmul(o, lhsT=tl[s][0:N1, kk, ch * 128:(ch + 1) * 128], rhs=w2t[0:N1, 0:NK1], start=True, stop=False),
                           reads=[b_tl[s], b_w2], writes=[pbufs[pb]])
                    pg.add('pe', lambda h, o=o, s=s, kk=kk, ch=ch: h.matmul(o, lhsT=tl[s][0:N1, kk, 256 + ch * 128:256 + (ch + 1) * 128], rhs=w2t[0:N1, NK1:2 * NK1], start=False, stop=True),
                           reads=[b_tl[s], b_w2], writes=[pbufs[pb]])
                if pos == K2G - 1:
                    for ch in range(2):
                        pb = 2 * (grp % 2) + ch
                        src = bank(pb)[:, 0:K2G * NK1]
                        if kind == 'P':
                            dst = foT[:, ch, :].rearrange("p (k1 k2) -> p k1 k2", k2=128)[:, :, grp * K2G:(grp + 1) * K2G]
                            sv = src.rearrange("p (k2 k1) -> p k1 k2", k1=NK1)
                        else:
                            AA = K2G // N1
                            dst = foT[:, ch, :].rearrange("p (jj k1 a) -> p jj k1 a", k1=NK1, a=128 // N1)[:, :, :, grp * AA:(grp + 1) * AA]
                            sv = src.rearrange("p (a jj k1) -> p jj k1 a", jj=N1, k1=NK1)
                        pg.add('act', lambda h, dst=dst, sv=sv: h.copy(out=dst, in_=sv), reads=[pbufs[pb]], writes=[b_foT])
        pg.barrier()
        stop_at('F2')
        A.pop()

        A.push()
        NKC = 8 if N1 >= 16 else 1
        CH = N1 // NKC
        KTg = A.alloc([P, N1, 128], BF16); Vg = A.alloc([P, N1, 128], BF16)
        stg = [A.alloc([P, CH, 512], BF16) for _ in range(2)]; b_stg = [Buf("stg%d" % i) for i in range(2)]
        b_KT = [Buf("KT%d" % i) for i in range(NKC)]; b_V = [Buf("V%d" % i) for i in range(NKC)]
        NPT = 4
        PT = [A.alloc([P, 2, QB], BF16) for _ in range(NPT)]; b_PT = [Buf("PT%d" % i) for i in range(NPT)]
        racc = [A.alloc([P, 2, QB], F32) for _ in range(2)]; b_racc = [Buf("racc%d" % i) for i in range(2)]
        rs = A.alloc([P, QB], F32); b_rs = Buf("rs")
        rinvb = A.alloc([P, QB], F32); b_rinvb = Buf("rinvb")
        ones_f = A.alloc([P, P], F32); b_ones = Buf("ones_f")
        pg.add('dve', lambda h: h.memset(ones_f, 1.0), [], [b_ones])
        scale = float(1.0 / np.sqrt(128.0))
        b_S = [Buf("Sp0"), Buf("Sp1")]
        b_accO = [Buf("accO0"), Buf("accO1")]
        b_rsb = [Buf("rsb0"), Buf("rsb1")]
        items = []
        blk = 0
        for g in range(2):
            for hl in range(3):
                for qb in range(NQB):
                    for jp in range(N1 // 2):
                        items.append(dict(g=g, hq=3 * g + hl, qb=qb, jp=jp, aset=blk % 2, newg=(hl == 0 and qb == 0 and jp == 0),
                                          last=(jp == N1 // 2 - 1)))
                    blk += 1
        cnt = {'sit': 0, 'pit': 0}
        pend = []

        def emit_S(it):
            g = it['g']; hq = it['hq']; qb = it['qb']; jp = it['jp']
            if it['newg']:
                for c4 in range(NKC):
                    st_ = (g * NKC + c4) % 2
                    dma('sp', stg[st_], KVsc[:, c4 * CH:(c4 + 1) * CH, :], [], [b_stg[st_]], b_stg[st_])
                    pg.add('pool', lambda h, c4=c4, st_=st_, g=g: h.tensor_copy(out=KTg[:, c4 * CH:(c4 + 1) * CH, :], in_=stg[st_][:, :, g * 128:(g + 1) * 128]),
                           reads=[b_stg[st_]], writes=[b_KT[c4]])
                    pg.add('dve', lambda h, c4=c4, st_=st_, g=g: h.tensor_copy(out=Vg[:, c4 * CH:(c4 + 1) * CH, :], in_=stg[st_][:, :, 256 + g * 128:256 + (g + 1) * 128]),
                           reads=[b_stg[st_]], writes=[b_V[c4]])
            sp_ = cnt['sit'] % 2; cnt['sit'] += 1
            it['sp'] = sp_
            qcols = QT[:, hq, qb * QB:(qb + 1) * QB]
            for u in range(2):
                j = 2 * jp + u
                pg.add('pe', lambda h, sp_=sp_, u=u, j=j, qcols=qcols: h.matmul(bank(2 * sp_ + u)[:, 0:QB], lhsT=KTg[:, j, :], rhs=qcols, start=True, stop=True),
                       reads=[b_KT[j // CH], b_QT[hq][qb]], writes=[b_S[sp_]])

        def emit_rest(it):
            g = it['g']; hq = it['hq']; qb = it['qb']; jp = it['jp']; aset = it['aset']; sp_ = it['sp']
            pt = cnt['pit'] % NPT; cnt['pit'] += 1
            qcols = QT[:, hq, qb * QB:(qb + 1) * QB]
            accO = bank(4 + aset)[:, 0:QB]
            sview = ps[:, (2 * sp_) * 512:(2 * sp_ + 2) * 512].rearrange("p (u c) -> p u c", u=2)[:, :, 0:QB]
            pg.add('act', lambda h, sview=sview, pt=pt: h.activation(out=PT[pt], in_=sview, func=AF.Exp, scale=scale),
                   reads=[b_S[sp_]], writes=[b_PT[pt]])
            for u in range(2):
                j = 2 * jp + u
                pg.add('pe', lambda h, pt=pt, u=u, j=j, accO=accO: h.matmul(accO, lhsT=Vg[:, j, :], rhs=PT[pt][:, u, :],
                                                                          start=(j == 0), stop=(j == N1 - 1)),
                       reads=[b_PT[pt], b_V[j // CH]], writes=[b_accO[aset]])
            if jp == 0:
                pg.add('dve', lambda h, pt=pt, aset=aset: h.tensor_copy(out=racc[aset], in_=PT[pt]),
                       reads=[b_PT[pt]], writes=[b_racc[aset]])
            else:
                pg.add('dve', lambda h, pt=pt, aset=aset: h.tensor_tensor(out=racc[aset], in0=racc[aset], in1=PT[pt], op=ALU.add),
                       reads=[b_PT[pt], b_racc[aset]], writes=[b_racc[aset]])
            if pend:
                pend.pop()()

            def epilogue(aset=aset, accO=accO, qcols=qcols, hq=hq, qb=qb):
                pg.add('dve', lambda h: h.tensor_tensor(out=rs, in0=racc[aset][:, 0, :], in1=racc[aset][:, 1, :], op=ALU.add),
                       reads=[b_racc[aset]], writes=[b_rs])
                rsb = bank(6 + aset)[:, 0:QB]
                pg.add('pe', lambda h: h.matmul(rsb, lhsT=ones_f, rhs=rs, start=True, stop=True),
                       reads=[b_rs, b_ones], writes=[b_rsb[aset]])
                pg.add('dve', lambda h: h.reciprocal(out=rinvb, in_=rsb), reads=[b_rsb[aset]], writes=[b_rinvb])
                pg.add('dve', lambda h: h.tensor_tensor(out=qcols, in0=accO, in1=rinvb, op=ALU.mult),
                       reads=[b_accO[aset], b_rinvb], writes=[b_QT[hq][qb]])
            if it['last']:
                pend.append(epilogue)

        for i_, it_ in enumerate(items):
            if 'sp' not in it_:
                emit_S(it_)
            if i_ + 1 < len(items) and not items[i_ + 1]['newg']:
                emit_S(items[i_ + 1])
            emit_rest(it_)
        if pend:
            pend.pop()()
        pg.barrier()
        stop_at('A')
        A.pop()

        A.push()
        kcT = A.alloc([P, 8, 256], BF16); vc = A.alloc([P, 2, 4, 257], BF16)
        b_kcT = Buf("kcT"); b_vc = Buf("vc")
        wo = A.alloc([P, 8, D], BF16); wcq = A.alloc([P, 8, D], BF16); wco = A.alloc([P, 8, D], BF16)
        b_wo = Buf("wo"); b_wcq = Buf("wcq"); b_wco = Buf("wco")
        A.push()
        wckv = A.alloc([P, 8, 2048], BF16); b_wckv = Buf("wckv")
        load_w_bf16(wckv, w_ckv, b_wckv, nsplit=1)
        load_w_bf16(wo, w_out, b_wo); load_w_bf16(wcq, w_cq, b_wcq); load_w_bf16(wco, w_co, b_wco)
        gmem_b, b_gmem = gain_tile(g_mem, "gmem")
        mt = A.alloc([P, D], F32); b_mt = Buf("mt")
        mjunk = A.alloc([P, D], BF16); b_mjunk = Buf("mjunk")
        mss = A.alloc([P, 1], F32); b_mss = Buf("mss")
        mh = A.alloc([P, D], BF16); b_mh = Buf("mh")
        mhT = A.alloc([P, 8, 128], BF16); b_mhT = Buf("mhT")
        ksq = A.alloc([P, 4, 256], F32); b_ksq = Buf("ksq")
        kss = A.alloc([P, 4], F32); b_kss = Buf("kss")
        kcn = A.alloc([P, 4, 256], F32); b_kcn = Buf("kcn")
        kcb = A.alloc([P, D], BF16); b_kcb = Buf("kcb")
        kcTt = A.alloc([P, 8, 128], BF16); b_kcTt = Buf("kcTt")
        pg.add('dve', lambda h: h.memset(vc[:, :, :, 256:257], 1.0), [], [b_vc])
        for mc in range(2):
            dma('sp', mt, sq_['mem'][mc * P:(mc + 1) * P, :], [], [b_mt], b_mt)
            rmsnorm_tile(mt, gmem_b, mh, mjunk, mss, b_mt, b_mh, b_mjunk, b_mss, b_gmem)
            transpose8(mh, b_mh, 4, mhT, b_mhT)
            for nb in range(4):
                for kc in range(8):
                    pg.add('pe', lambda h, nb=nb, kc=kc: h.matmul(bank(nb), lhsT=mhT[:, kc, :], rhs=wckv[:, kc, nb * 512:(nb + 1) * 512], start=(kc == 0), stop=(kc == 7)),
                           reads=[b_mhT, b_wckv], writes=[pbufs[nb]])
            pg.add('act', lambda h, mc=mc: h.copy(out=vc[:, mc, :, 0:256], in_=bank(2, 2).rearrange("p (h d) -> p h d", d=256)), reads=[pbufs[2], pbufs[3]], writes=[b_vc])
            k3 = bank(0, 2).rearrange("p (h d) -> p h d", d=256)
            for hh in range(4):
                pg.add('act', lambda h, k3=k3, hh=hh: h.activation(out=ksq[:, hh, :], in_=k3[:, hh, :], func=AF.Square, scale=1.0 / 16.0, accum_out=kss[:, hh:hh + 1]),
                       reads=[pbufs[0], pbufs[1]], writes=[b_ksq, b_kss])
            for hh in range(4):
                rstd_ops(kss[:, hh:hh + 1], 1, [b_kss], [b_kss])
            pg.add('dve', lambda h, k3=k3: h.tensor_tensor(out=kcn, in0=k3, in1=gck_b, op=ALU.mult), reads=[pbufs[0], pbufs[1], cb[10], b_ksq], writes=[b_kcn])
            pg.add('dve', lambda h: h.tensor_tensor(out=kcb.rearrange("p (h d) -> p h d", d=256), in0=kcn, in1=kss.unsqueeze(2).to_broadcast([P, 4, 256]), op=ALU.mult),
                   reads=[b_kcn, b_kss], writes=[b_kcb])
            transpose8(kcb, b_kcb, 5, kcTt, b_kcTt)
            pg.add('dve', lambda h, mc=mc: h.tensor_copy(out=kcT[:, :, mc * 128:(mc + 1) * 128], in_=kcTt), reads=[b_kcTt], writes=[b_kcT])
        pg.barrier()
        stop_at('M')
        A.pop()

        NS2 = 2
        gcross_b, b_gcross = gain_tile(g_cross, "gcross")
        xt1 = [A.alloc([P, D], F32) for _ in range(NS2)]; b_xt1 = [Buf("xt1_%d" % i) for i in range(NS2)]
        x1 = [A.alloc([P, D], F32) for _ in range(NS2)]; b_x1 = [Buf("x1_%d" % i) for i in range(NS2)]
        x2 = [A.alloc([P, D], F32) for _ in range(NS2)]; b_x2 = [Buf("x2_%d" % i) for i in range(NS2)]
        pj = A.alloc([P, D], BF16); b_pj = Buf("pj")
        pss = [A.alloc([P, 1], F32) for _ in range(NS2)]; b_pss = [Buf("pss%d" % i) for i in range(NS2)]

        def two(shape, dt, name):
            return [A.alloc(shape, dt) for _ in range(2)], [Buf("%s%d" % (name, i)) for i in range(2)]
        xh1, b_xh1 = two([P, D], BF16, "xh1"); xh1T, b_xh1T = two([P, 8, 128], BF16, "xh1T")
        qsq, b_qsq = two([P, 4, 256], F32, "qsq"); qss, b_qss = two([P, 4, 16], F32, "qss")
        qcn, b_qcn = two([P, 4, 256], F32, "qcn"); qcb, b_qcb = two([P, D], BF16, "qcb")
        qcT, b_qcT = two([P, 8, 128], BF16, "qcT"); PTc, b_PTc = two([P, 4, 2, 128], BF16, "PTc")
        rsi, b_rsi = two([P, 4], F32, "rsi"); ocb, b_ocb = two([P, D], BF16, "ocb"); ocT, b_ocT = two([P, 8, 128], BF16, "ocT")
        cscale = 1.0 / 16.0

        def st_load(t, p):
            dma('sp', xt1[p], sq_['xloc'][t], [], [b_xt1[p]], b_xt1[p])

        def st1(t, p):
            X = 2 * p
            tok = slice(t * P, (t + 1) * P)
            for nb in range(2):
                for kc in range(8):
                    if kc < 2:
                        lw = foT[:, kc, tok]; rb = [b_foT]
                    else:
                        lw = QT[:, kc - 2, tok]; rb = [b_QT[kc - 2][(t * P) // QB]]
                    pg.add('pe', lambda h, nb=nb, kc=kc, lw=lw: h.matmul(bank(X + nb), lhsT=lw, rhs=wo[:, kc, nb * 512:(nb + 1) * 512], start=(kc == 0), stop=(kc == 7)),
                           reads=rb + [b_wo], writes=[pbufs[X + nb]])

        def st2(t, p):
            X = 2 * p
            pg.add('dve', lambda h: h.tensor_tensor(out=x1[p], in0=bank(X, 2), in1=xt1[p], op=ALU.add), reads=[pbufs[X], pbufs[X + 1], b_xt1[p]], writes=[b_x1[p]])
            rmsnorm_tile(x1[p], gcross_b, xh1[p], pj, pss[p], b_x1[p], b_xh1[p], b_pj, b_pss[p], b_gcross)

        def st3(t, p):
            transpose8(xh1[p], b_xh1[p], 4 + p, xh1T[p], b_xh1T[p])

        def st4(t, p):
            X = 2 * p
            for nb in range(2):
                for kc in range(8):
                    pg.add('pe', lambda h, nb=nb, kc=kc: h.matmul(bank(X + nb), lhsT=xh1T[p][:, kc, :], rhs=wcq[:, kc, nb * 512:(nb + 1) * 512], start=(kc == 0), stop=(kc == 7)),
                           reads=[b_xh1T[p], b_wcq], writes=[pbufs[X + nb]])

        def st5(t, p):
            X = 2 * p
            q3 = bank(X, 2).rearrange("p (h d) -> p h d", d=256)
            for hh in range(4):
                pg.add('act', lambda h, hh=hh: h.activation(out=qsq[p][:, hh, :], in_=q3[:, hh, :], func=AF.Square, scale=1.0 / 16.0, accum_out=qss[p][:, hh, 0:1]),
                       reads=[pbufs[X], pbufs[X + 1]], writes=[b_qsq[p], b_qss[p]])
            for hh in range(4):
                rstd_ops(qss[p][:, hh, 0:1], 1, [b_qss[p]], [b_qss[p]])
            pg.add('dve', lambda h: h.tensor_tensor(out=qcn[p], in0=q3, in1=gcq_b, op=ALU.mult), reads=[pbufs[X], pbufs[X + 1], cb[9], b_qsq[p]], writes=[b_qcn[p]])
            pg.add('dve', lambda h: h.tensor_tensor(out=qcb[p].rearrange("p (h d) -> p h d", d=256), in0=qcn[p], in1=qss[p][:, :, 0:1].to_broadcast([P, 4, 256]), op=ALU.mult),
                   reads=[b_qcn[p], b_qss[p]], writes=[b_qcb[p]])

        def st6(t, p):
            transpose8(qcb[p], b_qcb[p], 4 + p, qcT[p], b_qcT[p])

        def st7(t, p):
            X = 2 * p
            for hh in range(4):
                for mc in range(2):
                    o = ps[:, X * 512 + (hh * 2 + mc) * 128: X * 512 + (hh * 2 + mc + 1) * 128]
                    for dc in range(2):
                        pg.add('pe', lambda h, o=o, hh=hh, mc=mc, dc=dc: h.matmul(o, lhsT=kcT[:, 2 * hh + dc, mc * 128:(mc + 1) * 128], rhs=qcT[p][:, 2 * hh + dc, :],
                                                                                  start=(dc == 0), stop=(dc == 1)),
                               reads=[b_kcT, b_qcT[p]], writes=[pbufs[X], pbufs[X + 1]])
            pg.add('act', lambda h: h.activation(out=PTc[p], in_=bank(X, 2).rearrange("p (h m t) -> p h m t", m=2, t=128), func=AF.Exp, scale=cscale),
                   reads=[pbufs[X], pbufs[X + 1]], writes=[b_PTc[p]])

        def st8(t, p):
            X = 2 * p; T = 4 + p
            for hh in range(4):
                for mc in range(2):
                    pg.add('pe', lambda h, hh=hh, mc=mc: h.matmul(ps[:, X * 512 + hh * 256:X * 512 + (hh + 1) * 256], lhsT=PTc[p][:, hh, mc, :], rhs=vc[:, mc, hh, 0:256], start=(mc == 0), stop=(mc == 1)),
                           reads=[b_PTc[p], b_vc], writes=[pbufs[X], pbufs[X + 1]])
                for mc in range(2):
                    pg.add('pe', lambda h, hh=hh, mc=mc: h.matmul(bank(T)[:, 256 + hh:256 + hh + 1], lhsT=PTc[p][:, hh, mc, :], rhs=vc[:, mc, hh, 256:257], start=(mc == 0), stop=(mc == 1)),
                           reads=[b_PTc[p], b_vc], writes=[pbufs[T]])

        def st9(t, p):
            X = 2 * p; T = 4 + p
            pg.add('dve', lambda h: h.reciprocal(out=rsi[p], in_=bank(T)[:, 256:260]), reads=[pbufs[T]], writes=[b_rsi[p]])
            pg.add('dve', lambda h: h.tensor_tensor(out=ocb[p].rearrange("p (h d) -> p h d", d=256), in0=bank(X, 2).rearrange("p (h d) -> p h d", d=256),
                                                    in1=rsi[p].unsqueeze(2).to_broadcast([P, 4, 256]), op=ALU.mult),
                   reads=[pbufs[X], pbufs[X + 1], b_rsi[p]], writes=[b_ocb[p]])

        def st10(t, p):
            transpose8(ocb[p], b_ocb[p], 4 + p, ocT[p], b_ocT[p])

        def st11(t, p):
            X = 2 * p
            for nb in range(2):
                for kc in range(8):
                    pg.add('pe', lambda h, nb=nb, kc=kc: h.matmul(bank(X + nb), lhsT=ocT[p][:, kc, :], rhs=wco[:, kc, nb * 512:(nb + 1) * 512], start=(kc == 0), stop=(kc == 7)),
                           reads=[b_ocT[p], b_wco], writes=[pbufs[X + nb]])

        def st12(t, p):
            X = 2 * p
            pg.add('dve', lambda h: h.tensor_tensor(out=x2[p], in0=bank(X, 2), in1=x1[p], op=ALU.add), reads=[pbufs[X], pbufs[X + 1], b_x1[p]], writes=[b_x2[p]])
            dma('pool', X2sc[sq_['tile0'] + t], x2[p], [b_x2[p]], [], b_x2[p])

        stages = [st1, st2, st3, st4, st5, st6, st7, st8, st9, st10, st11, st12]
        for t0 in range(0, NL, 2):
            pair = [(t0, 0)] + ([(t0 + 1, 1)] if t0 + 1 < NL else [])
            for (t, p) in pair:
                st_load(t, p)
            for f in stages:
                for (t, p) in pair:
                    f(t, p)
        pg.barrier()
        stop_at('P1')
        A.pop()
        A.pop()

    try:
        for si_, sqd in enumerate(seqs):
            do_seq(si_, sqd)
    except StopBuild:
        pg.emit(nc)
        return nc

    A.push()
    wup = A.alloc([P, 8, 4 * D], BF16); wdn = A.alloc([P, 32, D], BF16)
    b_wup = [Buf("wup%d" % i) for i in range(4)]; b_wdn = [Buf("wdn%d" % i) for i in range(4)]
    vup = w_up.rearrange("(kc p) n -> p kc n", p=P)
    vdn = w_down.rearrange("(fc p) n -> p fc n", p=P)
    for i in range(4):
        dma('pool', wup[:, :, i * 1024:(i + 1) * 1024], vup[:, :, i * 1024:(i + 1) * 1024], [], [b_wup[i]], b_wup[i])
    for i in range(4):
        dma('pool', wdn[:, i * 8:(i + 1) * 8, :], vdn[:, i * 8:(i + 1) * 8, :], [], [b_wdn[i]], b_wdn[i])
    TB = 2
    gmlp_b, b_gmlp = gain_tile(g_mlp, "gmlp")
    x2t = [A.alloc([P, D], F32) for _ in range(2 * TB)]; b_x2t = [Buf("x2t%d" % i) for i in range(2 * TB)]
    mj = A.alloc([P, D], BF16); b_mj = Buf("mj")
    ms = [A.alloc([P, 1], F32) for _ in range(2 * TB)]; b_ms = [Buf("ms%d" % i) for i in range(2 * TB)]
    xh2 = [A.alloc([P, D], BF16) for _ in range(2)]; b_xh2 = [Buf("xh2_%d" % i) for i in range(2)]
    xh2T = [A.alloc([P, 8, TB * P], BF16) for _ in range(2)]
    b_xh2T = [[Buf("xh2T%d_%d" % (i, k)) for k in range(TB)] for i in range(2)]
    uT = A.alloc([P, 32, TB * P], BF16); b_uT = [Buf("uT%d" % i) for i in range(32)]
    ur = [A.alloc([P, TB * P], F32) for _ in range(2)]; b_ur = [Buf("ur%d" % i) for i in range(2)]
    out_aps = []
    for sq_ in seqs:
        for t in range(sq_['NL']):
            out_aps.append(sq_['yout'][t])
    nb_ = NTL // TB
    assert NTL % TB == 0
    for b in range(nb_):
        bs = b % 2
        for k in range(TB):
            gt = b * TB + k
            sl = bs * TB + k
            dma('sp', x2t[sl], X2sc[gt], [], [b_x2t[sl]], b_x2t[sl])
            rmsnorm_tile(x2t[sl], gmlp_b, xh2[k % 2], mj, ms[sl], b_x2t[sl], b_xh2[k % 2], b_mj, b_ms[sl], b_gmlp)
            pv = bankb(6 + (k % 2))
            for kc in range(8):
                pg.add('pe', lambda h, kc=kc, k=k, pv=pv: h.transpose(out=pv[:, kc * 128:(kc + 1) * 128], in_=xh2[k % 2][:, kc * 128:(kc + 1) * 128], identity=identb),
                       reads=[b_xh2[k % 2], cb[0]], writes=[pbufs[6 + (k % 2)]])
            pg.add('act', lambda h, k=k, pv=pv, bs=bs: h.copy(out=xh2T[bs][:, :, k * P:(k + 1) * P], in_=pv.rearrange("p (a b) -> p a b", b=128)),
                   reads=[pbufs[6 + (k % 2)]], writes=[b_xh2T[bs][k]])
        for fc in range(32):
            pb = fc % 2
            for kc in range(8):
                pg.add('pe', lambda h, fc=fc, kc=kc, pb=pb, bs=bs: h.matmul(bank(pb)[:, 0:TB * P], lhsT=wup[:, kc, fc * 128:(fc + 1) * 128], rhs=xh2T[bs][:, kc, :],
                                                                            start=(kc == 0), stop=(kc == 7)),
                       reads=b_xh2T[bs] + [b_wup[fc // 8]], writes=[pbufs[pb]])
            pg.add('act', lambda h, fc=fc, pb=pb: h.activation(out=ur[pb], in_=bank(pb)[:, 0:TB * P], func=AF.Relu), reads=[pbufs[pb]], writes=[b_ur[pb]])
            eng = 'dve'
            pg.add(eng, lambda h, fc=fc, pb=pb: h.tensor_tensor(out=uT[:, fc, :], in0=ur[pb], in1=ur[pb], op=ALU.mult), reads=[b_ur[pb]], writes=[b_uT[fc]])
        for k in range(TB):
            gt = b * TB + k
            sl = bs * TB + k
            pb0 = 2 + 2 * (k % 2)
            for nb in range(2):
                for fc in range(32):
                    pg.add('pe', lambda h, nb=nb, fc=fc, k=k, pb0=pb0: h.matmul(bank(pb0 + nb), lhsT=uT[:, fc, k * P:(k + 1) * P], rhs=wdn[:, fc, nb * 512:(nb + 1) * 512],
                                                                                start=(fc == 0), stop=(fc == 31)),
                           reads=[b_uT[fc], b_wdn[fc // 8]], writes=[pbufs[pb0 + nb]])
            pg.add('dve', lambda h, k=k, sl=sl, pb0=pb0: h.tensor_tensor(out=x2t[sl], in0=bank(pb0, 2), in1=x2t[sl], op=ALU.add),
                   reads=[pbufs[pb0], pbufs[pb0 + 1], b_x2t[sl]], writes=[b_x2t[sl]])
            dma('pool', out_aps[gt], x2t[sl], [b_x2t[sl]], [], b_x2t[sl])
    A.pop()
    pg.emit(nc)
    return nc


def host_tables(cfg, core):
    bf = ml_dtypes.bfloat16
    N1P, N1S, LP = cfg.N1P, cfg.N1S, cfg.LP

    def rope_tab(S):
        n = np.arange(S)
        r = (n // 64).astype(np.float32); c = (n % 64).astype(np.float32)
        inv = (1.0 / (10000.0 ** (np.arange(0, 64, 2, dtype=np.float32) / 64.0))).astype(np.float32)
        ar = r[:, None] * inv[None, :]; ac = c[:, None] * inv[None, :]
        cr, sr, cc, sc = np.cos(ar), np.sin(ar), np.cos(ac), np.sin(ac)
        C = np.concatenate([cr, cr, cc, cc], axis=1)
        S_ = np.concatenate([-sr, sr, -sc, sc], axis=1)
        return np.concatenate([C, S_], axis=1).astype(np.float32)

    def dft_tab(S):
        n = np.arange(S, dtype=np.float64)[:, None]; k2 = np.arange(128, dtype=np.float64)[None, :]
        th = 2 * np.pi * ((n * k2) % S) / S
        fr = np.cos(th) / np.sqrt(S); fi = np.sin(th) / np.sqrt(S)
        return np.concatenate([fr, fi, -fi], axis=1).astype(np.float32).astype(bf)

    def w2_tab(N1, k1s):
        j = np.arange(N1, dtype=np.float64)[:, None]; k1 = np.asarray(k1s, dtype=np.float64)[None, :]
        th = 2 * np.pi * ((j * k1) % N1) / N1
        return np.concatenate([np.cos(th), -np.sin(th)], axis=1).astype(np.float32).astype(bf)

    ropeP = rope_tab(cfg.SP)
    c64 = np.arange(64, dtype=np.float64)
    th = 2 * np.pi * np.outer(c64, c64) / 64
    C64 = np.cos(th) / 8.0; S64 = np.sin(th) / 8.0
    Z = np.zeros((64, 64))
    bdc = np.block([[C64, Z], [Z, C64]]).astype(np.float32)
    bds = np.block([[S64, Z], [Z, S64]]).astype(np.float32)
    cst = np.zeros((128, 16), np.float32)
    cst[:, 0] = EPS; cst[:, 1] = -0.5; cst[:64, 2] = 1.0; cst[64:, 3] = 1.0; cst[:, 4] = 1.0
    return dict(
        ropeP=ropeP, ropePl=np.ascontiguousarray(ropeP[core * LP * 128:(core + 1) * LP * 128]), ropeS=rope_tab(cfg.SS),
        dftP=dft_tab(cfg.SP), dftS=dft_tab(cfg.SS),
        w2P=w2_tab(N1P, np.arange(core * LP, (core + 1) * LP)), w2S=w2_tab(N1S, np.arange(N1S)),
        identb=np.eye(128, dtype=np.float32).astype(bf), identf=np.eye(128, dtype=np.float32),
        bdc=bdc, bds=bds, cst=cst)


_CACHE = {}


def run(cfg, inputs):
    key = (cfg.N1P, cfg.N1S, cfg.NS)
    if key not in _CACHE:
        _CACHE[key] = build_program(cfg)
    nc = _CACHE[key]
    f = lambda a: np.ascontiguousarray(np.asarray(a, dtype=np.float32))
    xp = f(inputs["x_prompt"])[0]; xs = f(inputs["x_sample"])
    memp = f(inputs["mem_prompt"])[0]; mems = f(inputs["mem_sample"])
    LP, NS = cfg.LP, cfg.NS
    common = dict(
        xp=xp, memp=memp,
        g_mix=f(inputs["g_mix"])[0], g_cross=f(inputs["g_cross"])[0], g_mem=f(inputs["g_mem"])[0], g_mlp=f(inputs["g_mlp"])[0],
        g_q=f(inputs["g_q"])[0], g_k=f(inputs["g_k"])[0], g_cq=f(inputs["g_cq"])[0], g_ck=f(inputs["g_ck"])[0],
        w_in=f(inputs["w_in"])[0], w_f=f(inputs["w_fourier"])[0].reshape(256, 64),
        w_out=f(inputs["w_out"])[0], w_cq=f(inputs["w_cq"])[0], w_ckv=f(inputs["w_ckv"])[0], w_co=f(inputs["w_co"])[0],
        w_up=f(inputs["w_up"])[0], w_down=f(inputs["w_down"])[0])
    in_maps = []
    for c in range(NCORES):
        m = dict(common)
        m["xpl"] = np.ascontiguousarray(xp[c * LP * 128:(c + 1) * LP * 128])
        m["xs"] = np.ascontiguousarray(xs[c * NS:(c + 1) * NS])
        m["mems"] = np.ascontiguousarray(mems[c * NS:(c + 1) * NS])
        m.update(host_tables(cfg, c))
        in_maps.append(m)
    res = run_bass_kernel_spmd(nc, in_maps, core_ids=list(range(NCORES)))
    yp = np.concatenate([res.results[c]["yp"] for c in range(NCORES)], axis=0)[None]
    ys = np.concatenate([res.results[c]["ys"] for c in range(NCORES)], axis=0)
    return (yp.astype(np.float32), ys.astype(np.float32))


def kernel(**inputs):
    return run(Cfg(128, 16, 2), inputs)
```
